# Optimizing a Trainium2 kernel written in Bass

```python
import math
import jax, jax.numpy as jnp
from jax import lax
import numpy as np

D_MODEL = 1024
BATCH = 8
SEQ = 4096
DEPTH = 4

HEAD_DIM = 64
LRU_WIDTH = D_MODEL // 2
LRU_BLOCKS = LRU_WIDTH // HEAD_DIM
LRU_CONV = 4
LRU_C = 8.0
LRU_MIN_RAD = 0.9
LRU_MAX_RAD = 0.999
FOX_HEADS = (D_MODEL // 2) // HEAD_DIM
FOX_DIM = FOX_HEADS * HEAD_DIM
SWA_HEADS = (D_MODEL // 2) // HEAD_DIM
SWA_KV_HEADS = max(1, SWA_HEADS // 4)
SWA_DIM = SWA_HEADS * HEAD_DIM
SWA_WINDOW = 128
S5_WIDTH = D_MODEL // 2
S5_GROUP = 16
S5_GROUPS = S5_WIDTH // S5_GROUP
S5_STATE = 64
D_FF = 256 * ((8 * D_MODEL // 3 + 255) // 256)
PLE_DIM = 256
ROPE_THETA = 10000.0
QBLOCK = 128
EPS = 1e-6
MACARON = 0.5
OUT_SCALE = 0.5
N_EVEN = (DEPTH + 1) // 2
N_ODD = DEPTH // 2
EV_IN = 2 * LRU_WIDTH + 3 * FOX_DIM + FOX_HEADS
OD_IN = SWA_DIM + 2 * SWA_KV_HEADS * HEAD_DIM + S5_WIDTH
MIX_WIDTH = LRU_WIDTH + FOX_DIM

kernel_name = "hybrid_rglru_fox_swa_s5_macaron"


def rms_norm(x, g):
    x32 = x.astype(jnp.float32)
    y = x32 * lax.rsqrt(jnp.mean(x32 * x32, axis=-1, keepdims=True) + EPS)
    return (y * g.astype(jnp.float32)).astype(x.dtype)


def swiglu(x, wg, wu, wd):
    return (jax.nn.silu(x @ wg) * (x @ wu)) @ wd


def rope(x, pos):
    half = x.shape[-1] // 2
    inv = jnp.power(ROPE_THETA, -jnp.arange(half, dtype=jnp.float32) / half)
    ang = pos.astype(jnp.float32)[:, None] * inv[None, :]
    cos = jnp.cos(ang)[None, :, None, :]
    sin = jnp.sin(ang)[None, :, None, :]
    x32 = x.astype(jnp.float32)
    x1, x2 = x32[..., :half], x32[..., half:]
    return jnp.concatenate([x1 * cos - x2 * sin, x2 * cos + x1 * sin], axis=-1).astype(x.dtype)


def linear_scan_combine(left, right):
    a1, b1 = left
    a2, b2 = right
    return a1 * a2, a2 * b1 + b2


def rg_lru(xa, conv_w, conv_b, wa, ba, wx, bx, lam):
    B_, S_, W = xa.shape
    xp = jnp.pad(xa, ((0, 0), (LRU_CONV - 1, 0), (0, 0)))
    xc = conv_b
    for tap in range(LRU_CONV):
        xc = xc + xp[:, tap:tap + S_] * conv_w[tap]
    xh = xc.reshape(B_, S_, LRU_BLOCKS, W // LRU_BLOCKS)
    r = jax.nn.sigmoid((jnp.einsum('bshi,hij->bshj', xh, wa).reshape(B_, S_, W) + ba).astype(jnp.float32))
    i = jax.nn.sigmoid((jnp.einsum('bshi,hij->bshj', xh, wx).reshape(B_, S_, W) + bx).astype(jnp.float32))
    log_a = -LRU_C * r * jax.nn.softplus(lam.astype(jnp.float32))
    a = jnp.exp(log_a)
    b = jnp.sqrt(-jnp.expm1(2.0 * log_a)) * (i * xc.astype(jnp.float32))
    _, h = lax.associative_scan(linear_scan_combine, (a, b), axis=1)
    return h.astype(xa.dtype)


def fox_attention(q, k, v, f_logit, b_f, qn, kn):
    B_, S_, H, Dh = q.shape
    q = rms_norm(q, qn)
    k = rms_norm(k, kn)
    log_f = jax.nn.log_sigmoid(f_logit.astype(jnp.float32) + b_f.astype(jnp.float32))
    c = jnp.cumsum(log_f, axis=1).transpose(0, 2, 1)
    nb = S_ // QBLOCK
    qb = q.transpose(0, 2, 1, 3).reshape(B_, H, nb, QBLOCK, Dh).transpose(2, 0, 1, 3, 4)
    cb = c.reshape(B_, H, nb, QBLOCK).transpose(2, 0, 1, 3)
    kh = k.transpose(0, 2, 1, 3)
    vh = v.transpose(0, 2, 1, 3)
    kpos = jnp.arange(S_)
    scale = Dh ** -0.5

    def block(args):
        qi, ci, n = args
        s = jnp.einsum('bhqd,bhkd->bhqk', qi, kh).astype(jnp.float32) * scale
        s = s + ci[..., None] - c[:, :, None, :]
        qpos = n * QBLOCK + jnp.arange(QBLOCK)
        mask = kpos[None, :] <= qpos[:, None]
        s = jnp.where(mask, s, -jnp.inf)
        pr = jax.nn.softmax(s, axis=-1)
        return jnp.einsum('bhqk,bhkd->bhqd', pr.astype(vh.dtype), vh)

    o = lax.map(block, (qb, cb, jnp.arange(nb)))
    return o.transpose(1, 0, 3, 2, 4).reshape(B_, S_, H * Dh)


def swa_sink_attention(q, k, v, sinks, qn, kn):
    B_, S_, H, Dh = q.shape
    KVH = k.shape[2]
    G = H // KVH
    W = SWA_WINDOW
    nb = S_ // W
    pos = jnp.arange(S_)
    q = rope(rms_norm(q, qn), pos)
    k = rope(rms_norm(k, kn), pos)
    qb = q.reshape(B_, nb, W, KVH, G, Dh)

    def band(t):
        tp = jnp.pad(t, ((0, 0), (W, 0), (0, 0), (0, 0))).reshape(B_, nb + 1, W, KVH, Dh)
        return jnp.concatenate([tp[:, :-1], tp[:, 1:]], axis=2)

    kb, vb = band(k), band(v)
    s = jnp.einsum('bnqkgd,bnjkd->bnkgqj', qb, kb).astype(jnp.float32) * Dh ** -0.5
    qi = jnp.arange(W)[:, None]
    kj = jnp.arange(2 * W)[None, :]
    diff = qi + W - kj
    key_pos = jnp.arange(nb)[:, None, None] * W - W + kj[None]
    mask = (diff >= 0)[None] & (diff < SWA_WINDOW)[None] & (key_pos >= 0)
    s = jnp.where(mask[None, :, None, None], s, -jnp.inf)
    sink = sinks.astype(jnp.float32).reshape(KVH, G)[None, None, :, :, None, None]
    m = jnp.maximum(jnp.max(s, axis=-1, keepdims=True), sink)
    e = jnp.exp(s - m)
    pr = e / (jnp.sum(e, axis=-1, keepdims=True) + jnp.exp(sink - m))
    o = jnp.einsum('bnkgqj,bnjkd->bnqkgd', pr.astype(vb.dtype), vb)
    return o.reshape(B_, S_, H * Dh)


def s5_glu(u, lam_re, lam_im, log_dt, b_re, b_im, c_re, c_im, d, glu_w, glu_b):
    B_, S_, _ = u.shape
    f32 = jnp.float32
    u32 = u.astype(f32)
    ug = u32.reshape(B_, S_, S5_GROUPS, S5_GROUP)
    lam = lax.complex(lam_re.astype(f32), lam_im.astype(f32))
    dt = jnp.exp(log_dt.astype(f32))[:, None]
    lam_bar = jnp.exp(lam * dt)
    bmat = lax.complex(b_re.astype(f32), b_im.astype(f32))
    b_bar = ((lam_bar - 1.0) / lam)[..., None] * bmat
    bu = jnp.einsum('gpc,bsgc->bsgp', b_bar, ug.astype(jnp.complex64))
    a = jnp.broadcast_to(lam_bar[None, None], (1, S_, S5_GROUPS, S5_STATE))
    _, h = lax.associative_scan(linear_scan_combine, (a, bu), axis=1)
    cmat = lax.complex(c_re.astype(f32), c_im.astype(f32))
    y = jnp.real(jnp.einsum('gcp,bsgp->bsgc', cmat, h)).reshape(B_, S_, S5_WIDTH)
    y = y + d.astype(f32) * u32
    z = jax.nn.gelu(y).astype(u.dtype)
    return z * jax.nn.sigmoid(z @ glu_w + glu_b)


def even_mixer(h, w_in, conv_w, conv_b, wa, ba, wx, bx, lam, b_f, qn, kn, w_out):
    B_, S_, _ = h.shape
    z = h @ w_in
    o1 = LRU_WIDTH
    o2 = o1 + LRU_WIDTH
    o3 = o2 + FOX_DIM
    o4 = o3 + FOX_DIM
    o5 = o4 + FOX_DIM
    xa, ya, q, k, v, f = jnp.split(z, [o1, o2, o3, o4, o5], axis=-1)
    a_out = jax.nn.gelu(ya) * rg_lru(xa, conv_w, conv_b, wa, ba, wx, bx, lam)
    hd = (B_, S_, FOX_HEADS, HEAD_DIM)
    b_out = fox_attention(q.reshape(hd), k.reshape(hd), v.reshape(hd), f, b_f, qn, kn)
    return jnp.concatenate([a_out, b_out], axis=-1) @ w_out


def odd_mixer(h, w_in, qn, kn, sinks, lam_re, lam_im, log_dt, b_re, b_im, c_re, c_im, d,
              glu_w, glu_b, w_out):
    B_, S_, _ = h.shape
    z = h @ w_in
    kvd = SWA_KV_HEADS * HEAD_DIM
    o1 = SWA_DIM
    o2 = o1 + kvd
    o3 = o2 + kvd
    q, k, v, u = jnp.split(z, [o1, o2, o3], axis=-1)
    c_out = swa_sink_attention(q.reshape(B_, S_, SWA_HEADS, HEAD_DIM),
                               k.reshape(B_, S_, SWA_KV_HEADS, HEAD_DIM),
                               v.reshape(B_, S_, SWA_KV_HEADS, HEAD_DIM), sinks, qn, kn)
    d_out = s5_glu(u, lam_re, lam_im, log_dt, b_re, b_im, c_re, c_im, d, glu_w, glu_b)
    return jnp.concatenate([c_out, d_out], axis=-1) @ w_out


def setup_inputs(seed: int = 0) -> dict:
    key = jax.random.key(seed)
    ks = iter(jax.random.split(key, 64))
    f32 = jnp.float32

    def nrm(shape, scale):
        return jax.random.normal(next(ks), shape, f32) * scale

    def gain(shape):
        return 1.0 + 0.02 * jax.random.normal(next(ks), shape, f32)

    D, F, NE, NO = D_MODEL, D_FF, N_EVEN, N_ODD
    x = nrm((BATCH, SEQ, D), 1.0)
    p = nrm((DEPTH, BATCH, SEQ, PLE_DIM), 1.0)
    ffn1_norm = gain((DEPTH, D))
    ffn1_wg = nrm((DEPTH, D, F), D ** -0.5)
    ffn1_wu = nrm((DEPTH, D, F), D ** -0.5)
    ffn1_wd = nrm((DEPTH, F, D), F ** -0.5)
    mix_norm = gain((DEPTH, D))
    ffn2_norm = gain((DEPTH, D))
    ffn2_wg = nrm((DEPTH, D, F), D ** -0.5)
    ffn2_wu = nrm((DEPTH, D, F), D ** -0.5)
    ffn2_wd = nrm((DEPTH, F, D), F ** -0.5)
    ple_w = nrm((DEPTH, PLE_DIM, D), PLE_DIM ** -0.5)
    ple_norm = gain((DEPTH, D))
    ple_gate_norm = gain((DEPTH, D))
    ple_gate_w = nrm((DEPTH, D, D), D ** -0.5)
    ev_w_in = nrm((NE, D, EV_IN), D ** -0.5)
    lru_conv_w = nrm((NE, LRU_CONV, LRU_WIDTH), LRU_CONV ** -0.5)
    lru_conv_b = nrm((NE, LRU_WIDTH), 0.01)
    bs = LRU_WIDTH // LRU_BLOCKS
    lru_wa = nrm((NE, LRU_BLOCKS, bs, bs), bs ** -0.5)
    lru_ba = nrm((NE, LRU_WIDTH), 0.01)
    lru_wx = nrm((NE, LRU_BLOCKS, bs, bs), bs ** -0.5)
    lru_bx = nrm((NE, LRU_WIDTH), 0.01)
    unif = jax.random.uniform(next(ks), (NE, LRU_WIDTH), f32, LRU_MIN_RAD ** 2, LRU_MAX_RAD ** 2)
    lru_lambda = jnp.log(jnp.expm1(-0.5 * jnp.log(unif)))
    fox_bf = jax.random.uniform(next(ks), (NE, FOX_HEADS), f32, 1.0, 5.0)
    fox_q_norm = gain((NE, HEAD_DIM))
    fox_k_norm = gain((NE, HEAD_DIM))
    ev_w_out = nrm((NE, MIX_WIDTH, D), MIX_WIDTH ** -0.5 * OUT_SCALE)
    od_w_in = nrm((NO, D, OD_IN), D ** -0.5)
    swa_q_norm = gain((NO, HEAD_DIM))
    swa_k_norm = gain((NO, HEAD_DIM))
    swa_sinks = nrm((NO, SWA_HEADS), 0.5)
    n_idx = jnp.arange(S5_STATE, dtype=f32)
    s5_lambda_re = -0.5 + nrm((NO, S5_GROUPS, S5_STATE), 0.01)
    s5_lambda_im = jnp.pi * n_idx + nrm((NO, S5_GROUPS, S5_STATE), 0.01)
    s5_log_dt = jax.random.uniform(next(ks), (NO, S5_GROUPS), f32, math.log(1e-3), math.log(1e-1))
    s5_b_re = nrm((NO, S5_GROUPS, S5_STATE, S5_GROUP), (2 * S5_GROUP) ** -0.5)
    s5_b_im = nrm((NO, S5_GROUPS, S5_STATE, S5_GROUP), (2 * S5_GROUP) ** -0.5)
    s5_c_re = nrm((NO, S5_GROUPS, S5_GROUP, S5_STATE), S5_STATE ** -0.5)
    s5_c_im = nrm((NO, S5_GROUPS, S5_GROUP, S5_STATE), S5_STATE ** -0.5)
    s5_d = nrm((NO, S5_WIDTH), 1.0)
    s5_glu_w = nrm((NO, S5_WIDTH, S5_WIDTH), S5_WIDTH ** -0.5)
    s5_glu_b = nrm((NO, S5_WIDTH), 0.01)
    od_w_out = nrm((NO, MIX_WIDTH, D), MIX_WIDTH ** -0.5 * OUT_SCALE)
    return {
        "x": x, "p": p,
        "ffn1_norm": ffn1_norm, "ffn1_wg": ffn1_wg, "ffn1_wu": ffn1_wu, "ffn1_wd": ffn1_wd,
        "mix_norm": mix_norm,
        "ffn2_norm": ffn2_norm, "ffn2_wg": ffn2_wg, "ffn2_wu": ffn2_wu, "ffn2_wd": ffn2_wd,
        "ple_w": ple_w, "ple_norm": ple_norm, "ple_gate_norm": ple_gate_norm, "ple_gate_w": ple_gate_w,
        "ev_w_in": ev_w_in, "lru_conv_w": lru_conv_w, "lru_conv_b": lru_conv_b,
        "lru_wa": lru_wa, "lru_ba": lru_ba, "lru_wx": lru_wx, "lru_bx": lru_bx,
        "lru_lambda": lru_lambda, "fox_bf": fox_bf, "fox_q_norm": fox_q_norm,
        "fox_k_norm": fox_k_norm, "ev_w_out": ev_w_out,
        "od_w_in": od_w_in, "swa_q_norm": swa_q_norm, "swa_k_norm": swa_k_norm,
        "swa_sinks": swa_sinks, "s5_lambda_re": s5_lambda_re, "s5_lambda_im": s5_lambda_im,
        "s5_log_dt": s5_log_dt, "s5_b_re": s5_b_re, "s5_b_im": s5_b_im,
        "s5_c_re": s5_c_re, "s5_c_im": s5_c_im, "s5_d": s5_d,
        "s5_glu_w": s5_glu_w, "s5_glu_b": s5_glu_b, "od_w_out": od_w_out,
    }


def reference(x, p, ffn1_norm, ffn1_wg, ffn1_wu, ffn1_wd, mix_norm,
              ffn2_norm, ffn2_wg, ffn2_wu, ffn2_wd,
              ple_w, ple_norm, ple_gate_norm, ple_gate_w,
              ev_w_in, lru_conv_w, lru_conv_b, lru_wa, lru_ba, lru_wx, lru_bx,
              lru_lambda, fox_bf, fox_q_norm, fox_k_norm, ev_w_out,
              od_w_in, swa_q_norm, swa_k_norm, swa_sinks, s5_lambda_re, s5_lambda_im,
              s5_log_dt, s5_b_re, s5_b_im, s5_c_re, s5_c_im, s5_d,
              s5_glu_w, s5_glu_b, od_w_out):
    for i in range(DEPTH):
        x = x + MACARON * swiglu(rms_norm(x, ffn1_norm[i]), ffn1_wg[i], ffn1_wu[i], ffn1_wd[i])
        h = rms_norm(x, mix_norm[i])
        if i % 2 == 0:
            j = i // 2
            x = x + even_mixer(h, ev_w_in[j], lru_conv_w[j], lru_conv_b[j], lru_wa[j], lru_ba[j],
                               lru_wx[j], lru_bx[j], lru_lambda[j], fox_bf[j],
                               fox_q_norm[j], fox_k_norm[j], ev_w_out[j])
        else:
            j = i // 2
            x = x + odd_mixer(h, od_w_in[j], swa_q_norm[j], swa_k_norm[j], swa_sinks[j],
                              s5_lambda_re[j], s5_lambda_im[j], s5_log_dt[j], s5_b_re[j],
                              s5_b_im[j], s5_c_re[j], s5_c_im[j], s5_d[j],
                              s5_glu_w[j], s5_glu_b[j], od_w_out[j])
        x = x + MACARON * swiglu(rms_norm(x, ffn2_norm[i]), ffn2_wg[i], ffn2_wu[i], ffn2_wd[i])
        e = rms_norm(p[i] @ ple_w[i], ple_norm[i])
        g = jax.nn.sigmoid(rms_norm(x, ple_gate_norm[i]) @ ple_gate_w[i])
        x = x + g * e
    return x
```

```python
import math
from contextlib import ExitStack
import numpy as np
import concourse.bass as bass
import concourse.mybir as mybir
from concourse.bass_utils import run_bass_kernel_spmd

F32 = mybir.dt.float32
BF16 = mybir.dt.bfloat16
AF = mybir.ActivationFunctionType
ALU = mybir.AluOpType

D = 1024
FF = 2816
NFC = FF // 128
TT = 512
EPS = 1e-6
NSLOT = 8
EV_IN = 2568
OD_IN = 1280


class Res:
    __slots__ = ("w", "r", "name")

    def __init__(self, name=""):
        self.w = None
        self.r = {}
        self.name = name


class MK:
    def __init__(self, nc, es):
        self.nc = nc
        self.E = {"pe": nc.tensor, "act": nc.scalar, "dve": nc.vector, "pool": nc.gpsimd, "sp": nc.sync}
        self.csem = {e: es.enter_context(nc.semaphore("c_" + e)) for e in ["pe", "act", "dve", "pool"]}
        self.cnt = {e: 0 for e in self.csem}
        self.seen = {e: {} for e in self.E}
        self.dq = {q: [[es.enter_context(nc.semaphore("d_%s%d" % (q, i))), 0] for i in range(NSLOT)]
                   for q in ["sp", "act"]}
        self.dqi = {"sp": 0, "act": 0}
        self.nwait = 0

    def _wait(self, e, ev):
        sem, val = ev
        k = id(sem)
        if self.seen[e].get(k, 0) < val:
            self.E[e].wait_ge(sem, val)
            self.seen[e][k] = val
            self.nwait += 1

    def _deps(self, e, reads, writes):
        own = self.csem.get(e)
        for r in reads:
            if r.w is not None and not (e == "pe" and r.w[0] is own):
                self._wait(e, r.w)
        for w in writes:
            if w.w is not None and not (e == "pe" and w.w[0] is own):
                self._wait(e, w.w)
            for ev in w.r.values():
                if not (e == "pe" and ev[0] is own):
                    self._wait(e, ev)

    def _mark(self, ev, reads, writes):
        k = id(ev[0])
        for r in reads:
            old = r.r.get(k)
            if old is None or old[1] < ev[1]:
                r.r[k] = ev
        for w in writes:
            w.w = ev
            w.r = {}

    def op(self, e, fn, reads=(), writes=(), inc=True):
        self._deps(e, reads, writes)
        inst = fn()
        if inc:
            self.cnt[e] += 1
            inst.then_inc(self.csem[e], 1)
            ev = (self.csem[e], self.cnt[e])
        else:
            ev = (self.csem[e], self.cnt[e] + 1)
        self._mark(ev, reads, writes)
        return inst

    def barrier(self):
        for e in self.E:
            for o in self.csem:
                if o != e and self.cnt[o] > 0:
                    self._wait(e, (self.csem[o], self.cnt[o]))
            for q in self.dq:
                for slot in self.dq[q]:
                    if slot[1] > 0:
                        self._wait(e, (slot[0], slot[1]))

    def dma(self, q, out, in_, reads=(), writes=(), **kw):
        self._deps(q, reads, writes)
        slot = self.dq[q][self.dqi[q] % NSLOT]
        self.dqi[q] += 1
        if slot[1] > 0:
            self._wait(q, (slot[0], slot[1]))
        inst = self.E[q].dma_start(out=out, in_=in_, **kw)
        slot[1] += 16
        inst.then_inc(slot[0], 16)
        ev = (slot[0], slot[1])
        self._mark(ev, reads, writes)
        return ev


def build(S=4096, L=4, mixers=True, dbg=None):
    NT = S // TT
    NB = S // 128
    L2 = (L + 1) // 2
    nc = bass.Bass("TRN2", target_bir_lowering=False)
    es = ExitStack()
    mk = MK(nc, es)
    uid = [0]

    def din(name, shape, dt=F32):
        return nc.dram_tensor(name, list(shape), dt, kind="ExternalInput").ap()

    def dscr(name, shape, dt):
        return nc.dram_tensor(name, list(shape), dt, kind="Internal").ap()

    class Arena:
        def __init__(self):
            self.es = ExitStack()

        def sb(self, name, shape, dt):
            uid[0] += 1
            return self.es.enter_context(nc.sbuf_tensor("%s_%d" % (name, uid[0]), list(shape), dt))

        def close(self):
            mk.barrier()
            self.es.close()

    glob = Arena()

    xT_in = din("xT", [D, S])
    pT_in = din("pT", [L, 256, S])
    gains = din("gains", [128, 5, L, 8])
    w_ffn = {}
    for nm in ["ffn1_wg", "ffn1_wu", "ffn2_wg", "ffn2_wu"]:
        w_ffn[nm] = din(nm, [L, D, FF])
    for nm in ["ffn1_wd", "ffn2_wd"]:
        w_ffn[nm] = din(nm, [L, FF, D])
    ple_w = din("ple_w", [L, 256, D])
    ple_gate_w = din("ple_gate_w", [L, D, D])
    ident_in = din("ident", [128, 128])
    negmask_in = din("negmask", [128, 128])
    if mixers:
        ev_w_in = din("ev_w_in", [L2, D, EV_IN])
        ev_w_out = din("ev_w_out", [L2, D, D])
        lrup_in = din("lrup", [128, L2, 4, 8])
        lru_wbd = din("lru_wbd", [L2, 2, 4, 128, 128])
        foxp_in = din("foxp", [128, L2, 3])
        LO = max(1, L // 2)
        od_w_in = din("od_w_in", [LO, D, OD_IN])
        od_w_out = din("od_w_out", [LO, D, D])
        swap_in = din("swap", [128, LO, 12])
        ropec_in = din("ropec", [128, S])
        ropes_in = din("ropes", [128, S])
        bandmask_in = din("bandmask", [128, 256])
        iota_in = din("iota_ab", [128, 2, S])
        s5lam_in = din("s5lam", [128, LO, 16, 3])
        s5B_in = din("s5B", [LO, 2, 128, 16, 128])
        s5C_in = din("s5C", [LO, 2, 128, 16, 128])
        s5d_in = din("s5d", [128, LO, 4, 2])
        glu_w_in = din("s5_glu_w", [LO, 512, 512])
    yT_out = nc.dram_tensor("yT", [D, S], F32, kind="ExternalOutput").ap()

    xres_d = dscr("xres", [D, S], F32)
    wgu_s = dscr("wgu_s", [L, 4, NFC, 128, 8 * 128], BF16)
    wd_s = dscr("wd_s", [L, 2, 128, NFC * D], BF16)
    plew_s = dscr("plew_s", [L, 8, 128, 2 * 128], BF16)
    pgw_s = dscr("pgw_s", [L, 8, 128, 8 * 128], BF16)
    if mixers:
        evA_s = dscr("evA_s", [L2, 8, 128, 1024], BF16)
        evQK_s = dscr("evQK_s", [L2, 8, 128, 1024], BF16)
        evV_s = dscr("evV_s", [L2, 1, 128, 4096], BF16)
        evF_s = dscr("evF_s", [L2, 1, 128, 64], BF16)
        wout_s = dscr("wout_s", [L, 8, 128, 1024], BF16)
        zA_d = dscr("zA_d", [D, S], F32)
        qk_d = dscr("qk_d", [D, S], BF16)
        v_d = dscr("v_d", [S, 512], BF16)
        cs_d = dscr("cs_d", [3, 8, S], BF16)
        m_d = dscr("m_d", [D, S], BF16)
        od_s = dscr("od_s", [LO, 15, 128, 1024], BF16)
        glu_s = dscr("glu_s", [LO, 4, 128, 512], BF16)
    wres = Res("wscr")

    ones_bf = glob.sb("ones_bf", [128, 128], BF16)
    bd_ones = glob.sb("bd_ones", [128, 128], BF16)
    gains_sb = glob.sb("gains_sb", [128, 5, L, 8], F32)
    eps_col = glob.sb("eps_col", [128, 1], F32)
    ident = glob.sb("ident", [128, 128], F32)
    negmask = glob.sb("negmask", [128, 128], F32)
    cres = Res("consts")
    mk.op("dve", lambda: nc.vector.memset(ones_bf[:], 1.0), writes=[cres])
    mk.op("dve", lambda: nc.vector.memset(bd_ones[:], 0.0), writes=[cres])
    mk.op("dve", lambda: nc.vector.memset(bd_ones[0:64, 0:64], 1.0), writes=[cres])
    mk.op("dve", lambda: nc.vector.memset(bd_ones[64:128, 64:128], 1.0), writes=[cres])
    mk.op("dve", lambda: nc.vector.memset(eps_col[:], EPS), writes=[cres])
    mk.dma("sp", out=gains_sb[:], in_=gains, writes=[cres])
    mk.dma("sp", out=ident[:], in_=ident_in, writes=[cres])
    mk.dma("sp", out=negmask[:], in_=negmask_in, writes=[cres])
    if mixers:
        lrup = glob.sb("lrup", [128, L2, 4, 8], F32)
        foxp = glob.sb("foxp", [128, L2, 3], F32)
        mk.dma("sp", out=lrup[:], in_=lrup_in, writes=[cres])
        mk.dma("sp", out=foxp[:], in_=foxp_in, writes=[cres])
        swap = glob.sb("swap", [128, LO, 12], F32)
        mk.dma("sp", out=swap[:], in_=swap_in, writes=[cres])
        s5d = glob.sb("s5d", [128, LO, 4, 2], F32)
        mk.dma("sp", out=s5d[:], in_=s5d_in, writes=[cres])

    ps = [es.enter_context(nc.psum_tensor("ps%d" % i, [128, 512], F32)) for i in range(8)]
    psr = [Res("ps%d" % i) for i in range(8)]
    psi = [0]

    def next_ps(n=8):
        i = psi[0] % n
        psi[0] += 1
        return ps[i], psr[i]

    stg = [(glob.sb("stg%d" % i, [128, 4096], F32), Res()) for i in range(2)]
    stb = [(glob.sb("stb%d" % i, [128, 4096], BF16), Res()) for i in range(2)]
    pp = [0]
    pending = []

    def pump(n, engs=("dve", "act", "pool")):
        for _ in range(n):
            if not pending:
                return
            blk = pending.pop(0)
            blk(engs[pp[0] % len(engs)])

    def prep(src, K, N, cw, dst, swp=False):
        nk = K // 128
        nn = N // cw
        if nk * cw <= 4096:
            kb = nk
            nch = max(1, min(nn, 4096 // (nk * cw)))
        else:
            nch = 1
            kb = 4096 // cw
        srcv = src.rearrange("(k p) n -> p k n", p=128)
        for n0 in range(0, nn, nch):
            nb = min(nch, nn - n0)
            for k0 in range(0, nk, kb):
                kk = min(kb, nk - k0)
                pending.append(lambda e, n0=n0, nb=nb, k0=k0, kk=kk: prep_block(e, srcv, dst, cw, swp, n0, nb, k0, kk))

    def prep_block(e, srcv, dst, cw, swp, n0, nb, k0, kk):
        i = pp[0] % 2
        pp[0] += 1
        st, sr = stg[i]
        bt, br = stb[i]
        ne = kk * nb * cw
        stv = st[:, 0:ne].rearrange("p (k n) -> p k n", k=kk)
        if not swp:
            mk.dma("sp", out=stv, in_=srcv[:, k0:k0 + kk, n0 * cw:(n0 + nb) * cw], writes=[sr])
        else:
            sv4 = stv.rearrange("p k (g two c) -> p k g two c", two=2, c=32)
            iv4 = srcv[:, k0:k0 + kk, n0 * cw:(n0 + nb) * cw].rearrange("p k (g two c) -> p k g two c", two=2, c=32)
            for kx in range(kk):
                mk.dma("sp", out=sv4[:, kx, :, 0, :], in_=iv4[:, kx, :, 1, :], writes=[sr])
                mk.dma("sp", out=sv4[:, kx, :, 1, :], in_=iv4[:, kx, :, 0, :], writes=[sr])
        inv = st[:, 0:ne].rearrange("p (k n c) -> p n k c", k=kk, n=nb)
        outv = bt[:, 0:ne].rearrange("p (n k c) -> p n k c", n=nb, k=kk)
        if e == "act":
            mk.op("act", lambda: nc.scalar.copy(out=outv, in_=inv), reads=[sr], writes=[br])
        elif e == "dve":
            mk.op("dve", lambda: nc.vector.tensor_copy(out=outv, in_=inv), reads=[sr], writes=[br])
        else:
            mk.op("pool", lambda: nc.gpsimd.tensor_copy(out=outv, in_=inv), reads=[sr], writes=[br])
        dv = dst[n0:n0 + nb, :, k0 * cw:(k0 + kk) * cw].rearrange("n p x -> p n x")
        mk.dma("sp", out=dv, in_=bt[:, 0:ne].rearrange("p (n x) -> p n x", n=nb), reads=[br], writes=[wres])

    def enqueue_first_half(l):
        for j, nm in enumerate(["ffn1_wg", "ffn1_wu"]):
            prep(w_ffn[nm][l], D, FF, 128, wgu_s[l, j])
        prep(w_ffn["ffn1_wd"][l], FF, D, D, wd_s[l, 0:1])
        if mixers:
            j = l // 2
            if l % 2 == 0:
                prep(ev_w_in[j][:, 0:1024], D, 1024, 128, evA_s[j])
                prep(ev_w_in[j][:, 1024:2048], D, 1024, 128, evQK_s[j])
                prep(ev_w_in[j][:, 2048:2560], D, 512, 512, evV_s[j])
                prep(ev_w_in[j][:, 2560:2568], D, 8, 8, evF_s[j])
            else:
                prep(od_w_in[j][:, 0:512], D, 512, 128, od_s[j, 0:4])
                prep(od_w_in[j][:, 0:512], D, 512, 128, od_s[j, 4:8], swp=True)
                prep(od_w_in[j][:, 512:640], D, 128, 128, od_s[j, 8:9])
                prep(od_w_in[j][:, 512:640], D, 128, 128, od_s[j, 9:10], swp=True)
                prep(od_w_in[j][:, 640:768], D, 128, 128, od_s[j, 10:11])
                prep(od_w_in[j][:, 768:1280], D, 512, 128, od_s[j, 11:15])
                prep(glu_w_in[j], 512, 512, 128, glu_s[j])

    def enqueue_second_half(l):
        if mixers:
            j = l // 2
            prep((ev_w_out if l % 2 == 0 else od_w_out)[j], D, D, 128, wout_s[l])
        for j, nm in enumerate(["ffn2_wg", "ffn2_wu"]):
            prep(w_ffn[nm][l], D, FF, 128, wgu_s[l, 2 + j])
        prep(w_ffn["ffn2_wd"][l], FF, D, D, wd_s[l, 1:2])
        prep(ple_w[l], 256, D, 128, plew_s[l])
        prep(ple_gate_w[l], D, D, 128, pgw_s[l])

    enqueue_first_half(0)
    pump(10 ** 6)

    TL = {}

    def tl_alloc(a, kind):
        TL.clear()
        sets = []
        for b in range(2):
            d = {}
            d["hT"] = a.sb("hT%d" % b, [128, 8, TT], BF16)
            d["sq"] = a.sb("sq%d" % b, [128, 8, TT], BF16)
            d["rs"] = a.sb("rs%d" % b, [128, TT], F32)
            for nm in ["hr", "sqr", "rsr", "mr"]:
                d[nm] = Res(nm)
            if kind == "ple":
                d["pTf"] = a.sb("pTf%d" % b, [128, 2, TT], F32)
                d["pTb"] = a.sb("pTb%d" % b, [128, 2, TT], BF16)
                d["eT"] = a.sb("eT%d" % b, [128, 8, TT], F32)
                d["pTfr"] = Res()
                d["pTbr"] = Res()
                d["er"] = [Res("e%d" % k) for k in range(8)]
            sets.append(d)
        TL["sets"] = sets
        TL["XT"] = [(a.sb("xT_sb%d" % b, [128, 8, TT], F32), [Res("x%d" % k) for k in range(8)]) for b in range(3)]
        TL["hook"] = None
        TL["sg"] = [(a.sb("sg%d" % i, [128, TT], F32), Res()) for i in range(2)]
        TL["wgb"] = [(a.sb("wgb%d" % i, [128, 8, 128], BF16), Res()) for i in range(3)]
        if kind == "ffn":
            TL["actT"] = a.sb("actT", [128, NFC, TT], BF16)
            TL["actr"] = [Res("act%d" % i) for i in range(NFC)]
            TL["wub"] = [(a.sb("wub%d" % i, [128, 8, 128], BF16), Res()) for i in range(3)]
            TL["wdb"] = [(a.sb("wdb%d" % i, [128, D], BF16), Res()) for i in range(3)]
        if kind == "ple":
            TL["plewb"] = [(a.sb("plewb%d" % i, [128, 2, 128], BF16), Res()) for i in range(2)]
        if kind == "inproj":
            TL["wub"] = [(a.sb("wub%d" % i, [128, 8, 128], BF16), Res()) for i in range(3)]
            TL["zst"] = [(a.sb("zst%d" % i, [128, TT], F32), Res()) for i in range(2)]
            TL["qst"] = [(a.sb("qst%d" % i, [128, TT], BF16), Res()) for i in range(4)]
            TL["sqb4"] = [(a.sb("sqb%d" % i, [128, TT], BF16), Res()) for i in range(4)]
            TL["rs24"] = [(a.sb("rs2%d" % i, [128, TT], F32), Res()) for i in range(4)]
            TL["qa4"] = [(a.sb("qa%d" % i, [128, TT], F32), Res()) for i in range(2)]
            TL["wv"] = a.sb("wv", [128, 8, 512], BF16)
            TL["wf"] = a.sb("wf", [128, 8, 8], BF16)
            TL["ropeT"] = a.sb("ropeT", [128, 2, TT], F32)
            TL["ropeTr"] = Res("ropeT")
            TL["wv2"] = a.sb("wv2", [128, 8, 128], BF16)
            TL["wv2r"] = Res("wv2")
            TL["wvr"] = Res("wv")
            TL["wfr"] = Res("wf")
        TL["wctr"] = 0

    def sel(b):
        TL.update(TL["sets"][b % 2])
        TL["xT"], TL["xr"] = TL["XT"][b % 3]

    def run_hook():
        h = TL.get("hook")
        if h is not None:
            TL["hook"] = None
            h()

    def wslot(kind):
        i = TL["wctr"] % 3
        TL["wctr"] += 1
        return TL[kind][i]

    def rstd_from(src_reads, src_ap, scale):
        sq, sqr, rs, rsr = TL["sq"], TL["sqr"], TL["rs"], TL["rsr"]
        mk.op("act", lambda: nc.scalar.activation(out=sq[:], in_=src_ap, func=AF.Square),
              reads=src_reads, writes=[sqr])
        p, pr = next_ps()
        for k in range(8):
            mk.op("pe", lambda: nc.tensor.matmul(p[:], lhsT=ones_bf[:], rhs=sq[:, k, :], start=(k == 0), stop=(k == 7)),
                  reads=[sqr, cres], writes=[pr], inc=(k == 7))
        mk.op("act", lambda: nc.scalar.activation(out=rs[:], in_=p[:], func=AF.Ln, scale=scale, bias=eps_col[:]),
              reads=[pr, cres], writes=[rsr])
        mk.op("act", lambda: nc.scalar.activation(out=rs[:], in_=rs[:], func=AF.Exp, scale=-0.5), reads=[rsr], writes=[rsr])

    def rmsnorm_x(gi, l):
        xT, xr, hT, hr, rs, rsr = TL["xT"], TL["xr"], TL["hT"], TL["hr"], TL["rs"], TL["rsr"]
        rstd_from(xr, xT[:], 1.0 / D)
        for k in range(8):
            mk.op("dve", lambda: nc.vector.scalar_tensor_tensor(
                out=hT[:, k, :], in0=xT[:, k, :], scalar=gains_sb[:, gi, l, k:k + 1], in1=rs[:],
                op0=ALU.mult, op1=ALU.mult), reads=[xr[k], rsr, cres], writes=[hr])

    def mm8(p, pr, w, wr_, rhs_of_k, rhs_reads, nk=8):
        for k in range(nk):
            mk.op("pe", lambda: nc.tensor.matmul(p, lhsT=w[:, k, :], rhs=rhs_of_k(k), start=(k == 0), stop=(k == nk - 1)),
                  reads=[wr_] + rhs_reads, writes=[pr], inc=(k == nk - 1))

    def ffn(l, which, part):
        gi = 0 if which == 0 else 2
        if part == "pre":
            rmsnorm_x(gi, l)
            return
        xT, xr, hT, hr, actT, actr = TL["xT"], TL["xr"], TL["hT"], TL["hr"], TL["actT"], TL["actr"]
        for fc in range(NFC):
            i = TL["wctr"] % 3
            TL["wctr"] += 1
            wg, wgr = TL["wgb"][i]
            wu, wur = TL["wub"][i]
            mk.dma("sp", out=wg[:].rearrange("p k j -> p (k j)"), in_=wgu_s[l, 2 * which, fc], reads=[wres], writes=[wgr])
            mk.dma("sp", out=wu[:].rearrange("p k j -> p (k j)"), in_=wgu_s[l, 2 * which + 1, fc], reads=[wres], writes=[wur])
            pg, pgr = next_ps()
            pu, pur = next_ps()
            mm8(pg[:], pgr, wg, wgr, lambda k: hT[:, k, :], [hr])
            mm8(pu[:], pur, wu, wur, lambda k: hT[:, k, :], [hr])
            s, sr = TL["sg"][fc % 2]
            mk.op("act", lambda: nc.scalar.activation(out=s[:], in_=pg[:], func=AF.Silu), reads=[pgr], writes=[sr])
            mk.op("dve", lambda: nc.vector.tensor_tensor(out=actT[:, fc, :], in0=pu[:], in1=s[:], op=ALU.mult),
                  reads=[pur, sr], writes=[actr[fc]])
            if fc == 1:
                run_hook()
        for fc in range(NFC):
            wd, wdr = wslot("wdb")
            mk.dma("sp", out=wd[:], in_=wd_s[l, which, :, fc * D:(fc + 1) * D], reads=[wres], writes=[wdr])
            for dc in range(8):
                mk.op("pe", lambda: nc.tensor.matmul(ps[dc][:], lhsT=wd[:, dc * 128:(dc + 1) * 128], rhs=actT[:, fc, :],
                                                     start=(fc == 0), stop=(fc == NFC - 1)),
                      reads=[wdr, actr[fc]], writes=[psr[dc]], inc=(dc == 7 or fc == NFC - 1))
        for dc in range(8):
            mk.op("dve", lambda: nc.vector.scalar_tensor_tensor(
                out=xT[:, dc, :], in0=ps[dc][:], scalar=0.5, in1=xT[:, dc, :], op0=ALU.mult, op1=ALU.add),
                reads=[psr[dc], xr[dc]], writes=[xr[dc]])

    def ple(l, t, part):
        xT, xr, hT, hr, eT, er = TL["xT"], TL["xr"], TL["hT"], TL["hr"], TL["eT"], TL["er"]
        pTf, pTb, rs, rsr = TL["pTf"], TL["pTb"], TL["rs"], TL["rsr"]
        if part == "pre":
            ple_pre(l, t)
            return
        ple_main(l)

    def ple_pre(l, t):
        xT, xr, hT, hr, eT, er = TL["xT"], TL["xr"], TL["hT"], TL["hr"], TL["eT"], TL["er"]
        pTf, pTb, rs, rsr = TL["pTf"], TL["pTb"], TL["rs"], TL["rsr"]
        mk.dma("sp", out=pTf[:], in_=pT_in[l, :, t * TT:(t + 1) * TT].rearrange("(k p) s -> p k s", p=128), writes=[TL["pTfr"]])
        mk.op("act", lambda: nc.scalar.copy(out=pTb[:], in_=pTf[:]), reads=[TL["pTfr"]], writes=[TL["pTbr"]])
        for dc in range(8):
            w, wr_ = TL["plewb"][dc % 2]
            mk.dma("sp", out=w[:].rearrange("p k j -> p (k j)"), in_=plew_s[l, dc], reads=[wres], writes=[wr_])
            p, pr = next_ps()
            mm8(p[:], pr, w, wr_, lambda k: pTb[:, k, :], [TL["pTbr"]], nk=2)
            mk.op("act", lambda: nc.scalar.copy(out=eT[:, dc, :], in_=p[:]), reads=[pr], writes=[er[dc]])
        rstd_from(er, eT[:], 1.0 / D)
        for k in range(8):
            mk.op("dve", lambda: nc.vector.scalar_tensor_tensor(
                out=eT[:, k, :], in0=eT[:, k, :], scalar=gains_sb[:, 3, l, k:k + 1], in1=rs[:],
                op0=ALU.mult, op1=ALU.mult), reads=[er[k], rsr, cres], writes=[er[k]])
        rmsnorm_x(4, l)

    def ple_main(l):
        xT, xr, hT, hr, eT, er = TL["xT"], TL["xr"], TL["hT"], TL["hr"], TL["eT"], TL["er"]
        for dc in range(8):
            wg, wgr = wslot("wgb")
            mk.dma("sp", out=wg[:].rearrange("p k j -> p (k j)"), in_=pgw_s[l, dc], reads=[wres], writes=[wgr])
            p, pr = next_ps()
            mm8(p[:], pr, wg, wgr, lambda k: hT[:, k, :], [hr])
            s, sr = TL["sg"][dc % 2]
            mk.op("act", lambda: nc.scalar.activation(out=s[:], in_=p[:], func=AF.Sigmoid), reads=[pr], writes=[sr])
            mk.op("dve", lambda: nc.vector.tensor_tensor(out=s[:], in0=s[:], in1=eT[:, dc, :], op=ALU.mult),
                  reads=[sr, er[dc]], writes=[sr])
            mk.op("dve", lambda: nc.vector.tensor_tensor(out=xT[:, dc, :], in0=xT[:, dc, :], in1=s[:], op=ALU.add),
                  reads=[sr, xr[dc]], writes=[xr[dc]])
            if dc == 1:
                run_hook()

    def outproj(l, t):
        xT, xr, mT, mr = TL["xT"], TL["xr"], TL["hT"], TL["hr"]
        mk.dma("sp", out=mT[:], in_=m_d[:, t * TT:(t + 1) * TT].rearrange("(k p) s -> p k s", p=128), writes=[mr])
        for dc in range(8):
            wg, wgr = wslot("wgb")
            mk.dma("sp", out=wg[:].rearrange("p k j -> p (k j)"), in_=wout_s[l, dc], reads=[wres], writes=[wgr])
            p, pr = next_ps()
            mm8(p[:], pr, wg, wgr, lambda k: mT[:, k, :], [mr])
            mk.op("dve", lambda: nc.vector.tensor_tensor(out=xT[:, dc, :], in0=p[:], in1=xT[:, dc, :], op=ALU.add),
                  reads=[pr, xr[dc]], writes=[xr[dc]])

    def inproj_even(l, t, f_sb, f_r):
        j = l // 2
        hT, hr = TL["hT"], TL["hr"]
        tsl = slice(t * TT, (t + 1) * TT)
        for c in range(8):
            wg, wgr = wslot("wgb")
            mk.dma("sp", out=wg[:].rearrange("p k j -> p (k j)"), in_=evA_s[j, c], reads=[wres], writes=[wgr])
            p, pr = next_ps()
            mm8(p[:], pr, wg, wgr, lambda k: hT[:, k, :], [hr])
            z, zr = TL["zst"][c % 2]
            mk.op("act", lambda: nc.scalar.copy(out=z[:], in_=p[:]), reads=[pr], writes=[zr])
            mk.dma("act", out=zA_d[c * 128:(c + 1) * 128, tsl], in_=z[:], reads=[zr])
        for g0 in range(0, 8, 4):
            grp = list(range(g0, g0 + 4))
            pm = {}
            for c in grp:
                wg, wgr = wslot("wgb")
                mk.dma("sp", out=wg[:].rearrange("p k j -> p (k j)"), in_=evQK_s[j, c], reads=[wres], writes=[wgr])
                pm[c] = next_ps()
                mm8(pm[c][0][:], pm[c][1], wg, wgr, lambda k: hT[:, k, :], [hr])
            for c in grp:
                sqb, sqbr = TL["sqb4"][c % 4]
                mk.op("act", lambda: nc.scalar.activation(out=sqb[:], in_=pm[c][0][:], func=AF.Square), reads=[pm[c][1]], writes=[sqbr])
            p2s = {}
            for c in grp:
                sqb, sqbr = TL["sqb4"][c % 4]
                p2s[c] = next_ps()
                mk.op("pe", lambda: nc.tensor.matmul(p2s[c][0][:], lhsT=bd_ones[:], rhs=sqb[:], start=True, stop=True),
                      reads=[sqbr, cres], writes=[p2s[c][1]])
            for c in grp:
                rs2, rs2r = TL["rs24"][c % 4]
                mk.op("act", lambda: nc.scalar.activation(out=rs2[:], in_=p2s[c][0][:], func=AF.Ln, scale=1.0 / 64, bias=eps_col[:]),
                      reads=[p2s[c][1], cres], writes=[rs2r])
            for c in grp:
                rs2, rs2r = TL["rs24"][c % 4]
                mk.op("act", lambda: nc.scalar.activation(out=rs2[:], in_=rs2[:], func=AF.Exp, scale=-0.5), reads=[rs2r], writes=[rs2r])
            for c in grp:
                rs2, rs2r = TL["rs24"][c % 4]
                q, qr = TL["qst"][c % 4]
                gcol = foxp[:, j, (0 if c < 4 else 1):(1 if c < 4 else 2)]
                mk.op("dve", lambda: nc.vector.scalar_tensor_tensor(out=q[:], in0=pm[c][0][:], scalar=gcol, in1=rs2[:],
                                                                    op0=ALU.mult, op1=ALU.mult),
                      reads=[pm[c][1], rs2r, cres], writes=[qr])
                mk.dma("act", out=qk_d[c * 128:(c + 1) * 128, tsl], in_=q[:], reads=[qr])
        wv, wvr, wf, wfr = TL["wv"], TL["wvr"], TL["wf"], TL["wfr"]
        if t == 0:
            mk.dma("sp", out=wv[:].rearrange("p k j -> p (k j)"), in_=evV_s[j, 0], reads=[wres], writes=[wvr])
            mk.dma("sp", out=wf[:].rearrange("p k j -> p (k j)"), in_=evF_s[j, 0], reads=[wres], writes=[wfr])
        for tb in range(TT // 128):
            p, pr = next_ps()
            for k in range(8):
                mk.op("pe", lambda: nc.tensor.matmul(p[:], lhsT=hT[:, k, tb * 128:(tb + 1) * 128], rhs=wv[:, k, :],
                                                     start=(k == 0), stop=(k == 7)),
                      reads=[wvr, hr], writes=[pr], inc=(k == 7))
            q, qr = TL["qst"][tb % 2]
            mk.op("act", lambda: nc.scalar.copy(out=q[:], in_=p[:]), reads=[pr], writes=[qr])
            mk.dma("act", out=v_d[t * TT + tb * 128:t * TT + (tb + 1) * 128, :], in_=q[:], reads=[qr])
        p, pr = next_ps()
        for k in range(8):
            mk.op("pe", lambda: nc.tensor.matmul(p[0:8, :], lhsT=wf[:, k, :], rhs=hT[:, k, :], start=(k == 0), stop=(k == 7)),
                  reads=[wfr, hr], writes=[pr], inc=(k == 7))
        mk.op("act", lambda: nc.scalar.copy(out=f_sb[:, tsl], in_=p[0:8, :]), reads=[pr], writes=[f_r])


    def inproj_odd(l, t):
        j = l // 2
        hT, hr = TL["hT"], TL["hr"]
        tsl = slice(t * TT, (t + 1) * TT)
        ropeT, ropeTr = TL["ropeT"], TL["ropeTr"]
        mk.dma("sp", out=ropeT[:, 0, :], in_=ropec_in[:, tsl], writes=[ropeTr])
        mk.dma("sp", out=ropeT[:, 1, :], in_=ropes_in[:, tsl], writes=[ropeTr])
        for grp in [[0, 1], [2, 3], [4]]:
            pm, pw_ = {}, {}
            for c in grp:
                wi, wsi = (c, 4 + c) if c < 4 else (8, 9)
                wg, wgr = wslot("wgb")
                mk.dma("sp", out=wg[:].rearrange("p k j -> p (k j)"), in_=od_s[j, wi], reads=[wres], writes=[wgr])
                wu, wur = wslot("wub")
                mk.dma("sp", out=wu[:].rearrange("p k j -> p (k j)"), in_=od_s[j, wsi], reads=[wres], writes=[wur])
                pm[c] = next_ps()
                mm8(pm[c][0][:], pm[c][1], wg, wgr, lambda k: hT[:, k, :], [hr])
                pw_[c] = next_ps()
                mm8(pw_[c][0][:], pw_[c][1], wu, wur, lambda k: hT[:, k, :], [hr])
            for c in grp:
                sqb, sqbr = TL["sqb4"][c % 4]
                mk.op("act", lambda: nc.scalar.activation(out=sqb[:], in_=pm[c][0][:], func=AF.Square), reads=[pm[c][1]], writes=[sqbr])
            p2s = {}
            for c in grp:
                sqb, sqbr = TL["sqb4"][c % 4]
                p2s[c] = next_ps()
                mk.op("pe", lambda: nc.tensor.matmul(p2s[c][0][:], lhsT=bd_ones[:], rhs=sqb[:], start=True, stop=True),
                      reads=[sqbr, cres], writes=[p2s[c][1]])
            for c in grp:
                rs2, rs2r = TL["rs24"][c % 4]
                mk.op("act", lambda: nc.scalar.activation(out=rs2[:], in_=p2s[c][0][:], func=AF.Ln, scale=1.0 / 64, bias=eps_col[:]),
                      reads=[p2s[c][1], cres], writes=[rs2r])
            for c in grp:
                rs2, rs2r = TL["rs24"][c % 4]
                mk.op("act", lambda: nc.scalar.activation(out=rs2[:], in_=rs2[:], func=AF.Exp, scale=-0.5), reads=[rs2r], writes=[rs2r])
            for c in grp:
                g0 = 0 if c < 4 else 2
                rs2, rs2r = TL["rs24"][c % 4]
                qa, qar = (TL["qa4"] + TL["zst"])[(2 * c) % 4]
                qb, qbr = (TL["qa4"] + TL["zst"])[(2 * c + 1) % 4]
                mk.op("dve", lambda: nc.vector.scalar_tensor_tensor(out=qa[:], in0=pm[c][0][:], scalar=swap[:, j, g0:g0 + 1], in1=rs2[:],
                                                                    op0=ALU.mult, op1=ALU.mult),
                      reads=[pm[c][1], rs2r, cres], writes=[qar])
                mk.op("dve", lambda: nc.vector.scalar_tensor_tensor(out=qb[:], in0=pw_[c][0][:], scalar=swap[:, j, g0 + 1:g0 + 2], in1=rs2[:],
                                                                    op0=ALU.mult, op1=ALU.mult),
                      reads=[pw_[c][1], rs2r, cres], writes=[qbr])
                mk.op("dve", lambda: nc.vector.tensor_tensor(out=qa[:], in0=qa[:], in1=ropeT[:, 0, :], op=ALU.mult),
                      reads=[qar, ropeTr], writes=[qar])
                mk.op("pool", lambda: nc.gpsimd.tensor_tensor(out=qb[:], in0=qb[:], in1=ropeT[:, 1, :], op=ALU.mult),
                      reads=[qbr, ropeTr], writes=[qbr])
                q, qr = TL["qst"][c % 4]
                mk.op("dve", lambda: nc.vector.tensor_tensor(out=q[:], in0=qa[:], in1=qb[:], op=ALU.add),
                      reads=[qar, qbr], writes=[qr])
                r0 = c * 128 if c < 4 else 512
                mk.dma("act", out=qk_d[r0:r0 + 128, tsl], in_=q[:], reads=[qr])
        wv2, wv2r = TL["wv2"], TL["wv2r"]
        if t == 0:
            mk.dma("sp", out=wv2[:].rearrange("p k j -> p (k j)"), in_=od_s[j, 10], reads=[wres], writes=[wv2r])
        for tb in range(TT // 128):
            p, pr = next_ps()
            for k in range(8):
                mk.op("pe", lambda: nc.tensor.matmul(p[:, 0:128], lhsT=hT[:, k, tb * 128:(tb + 1) * 128], rhs=wv2[:, k, :],
                                                     start=(k == 0), stop=(k == 7)),
                      reads=[wv2r, hr], writes=[pr], inc=(k == 7))
            q, qr = TL["qst"][tb % 2]
            mk.op("act", lambda: nc.scalar.copy(out=q[:, 0:128], in_=p[:, 0:128]), reads=[pr], writes=[qr])
            mk.dma("act", out=v_d[t * TT + tb * 128:t * TT + (tb + 1) * 128, 0:128], in_=q[:, 0:128], reads=[qr])
        for c in range(4):
            wg, wgr = wslot("wgb")
            mk.dma("sp", out=wg[:].rearrange("p k j -> p (k j)"), in_=od_s[j, 11 + c], reads=[wres], writes=[wgr])
            p, pr = next_ps()
            mm8(p[:], pr, wg, wgr, lambda k: hT[:, k, :], [hr])
            z, zr = TL["zst"][c % 2]
            mk.op("act", lambda: nc.scalar.copy(out=z[:], in_=p[:]), reads=[pr], writes=[zr])
            mk.dma("act", out=zA_d[c * 128:(c + 1) * 128, tsl], in_=z[:], reads=[zr])

    def swa_core(l):
        j = l // 2
        a = Arena()
        Kt = [(a.sb("sKt%d" % i, [64, S], BF16), Res()) for i in range(2)]
        Va = [(a.sb("sVa%d" % i, [128, NB, 128], BF16), Res()) for i in range(2)]
        Qh = [(a.sb("sQh%d" % i, [64, S], BF16), Res()) for i in range(2)]
        pt = [(a.sb("spt%d" % i, [128, 256], BF16), Res()) for i in range(4)]
        rd = [(a.sb("srd%d" % i, [128, 512], F32), Res()) for i in range(2)]
        obs = [(a.sb("sobs%d" % i, [64, 512], BF16), Res()) for i in range(2)]
        bmf = a.sb("bmf", [128, 256], F32)
        bm = a.sb("bm", [128, 256], BF16)
        esink = a.sb("esink", [128, 8], F32)
        R = {n: Res(n) for n in ["bmf", "bm", "esink"]}
        mk.dma("sp", out=bmf[:], in_=bandmask_in, writes=[R["bmf"]])
        mk.op("dve", lambda: nc.vector.tensor_copy(out=bm[:], in_=bmf[:]), reads=[R["bmf"]], writes=[R["bm"]])
        mk.op("act", lambda: nc.scalar.activation(out=esink[:], in_=swap[:, j, 4:12], func=AF.Exp), reads=[cres], writes=[R["esink"]])
        for i in range(2):
            mk.op("pool", lambda: nc.gpsimd.memset(Va[i][0][:, :, 64:128], 1.0), writes=[Va[i][1]])
        ptc = 0
        oq = 0
        for g in range(2):
            Kg, Kr = Kt[g % 2]
            V, Vr = Va[g % 2]
            mk.dma("sp", out=Kg[:], in_=qk_d[512 + g * 64:512 + (g + 1) * 64, :], writes=[Kr])
            mk.dma("sp", out=V[:, :, 0:64], in_=v_d[:, g * 64:(g + 1) * 64].rearrange("(j p) d -> p j d", p=128), writes=[Vr])
            for hh in range(4):
                h = g * 4 + hh
                Q, Qr = Qh[h % 2]
                mk.dma("sp", out=Q[:], in_=qk_d[h * 64:(h + 1) * 64, :], writes=[Qr])
                pump(2, ("act", "dve"))
                Pprev = None

                def sqk(J):
                    N = 256 if J < NB - 1 else 128
                    pS, pSr = next_ps(6)
                    mk.op("pe", lambda: nc.tensor.matmul(pS[:, 0:N], lhsT=Kg[:, J * 128:(J + 1) * 128],
                                                         rhs=Q[:, J * 128:J * 128 + N], start=True, stop=True),
                          reads=[Kr, Qr], writes=[pSr])
                    return pS, pSr, N
                SLA = 3
                sq_ = [sqk(J0) for J0 in range(min(SLA, NB))]
                for J in range(NB):
                    pS, pSr, N = sq_.pop(0)
                    if J + SLA < NB:
                        sq_.append(sqk(J + SLA))
                    P, Pr = pt[ptc % 4]
                    ptc += 1
                    mk.op("act", lambda: nc.scalar.activation(out=P[:, 0:N], in_=pS[:, 0:N], func=AF.Exp, scale=0.125),
                          reads=[pSr], writes=[Pr])
                    mk.op("dve", lambda: nc.vector.tensor_tensor(out=P[:, 0:N], in0=P[:, 0:N], in1=bm[:, 0:N], op=ALU.mult),
                          reads=[Pr, R["bm"]], writes=[Pr])
                    po, por = ps[6 + (oq % 2)], psr[6 + (oq % 2)]
                    reg = slice((J % 4) * 128, (J % 4 + 1) * 128)
                    if J > 0:
                        Pp, Ppr = Pprev
                        mk.op("pe", lambda: nc.tensor.matmul(po[:, reg], lhsT=V[:, J - 1, :], rhs=Pp[:, 128:256],
                                                             start=True, stop=False),
                              reads=[Vr, Ppr], writes=[por], inc=False)
                    mk.op("pe", lambda: nc.tensor.matmul(po[:, reg], lhsT=V[:, J, :], rhs=P[:, 0:128],
                                                         start=(J == 0), stop=True),
                          reads=[Vr, Pr], writes=[por], inc=True)
                    Pprev = (P, Pr)
                    if J % 4 == 3:
                        r_, rr_ = rd[oq % 2]
                        o_, or_ = obs[oq % 2]
                        mk.op("dve", lambda: nc.vector.tensor_scalar(out=r_[64:128, :], in0=po[64:128, :],
                                                                     scalar1=esink[64:128, h:h + 1], scalar2=None, op0=ALU.add),
                              reads=[por, R["esink"]], writes=[rr_])
                        mk.op("dve", lambda: nc.vector.reciprocal(out=r_[64:128, :], in_=r_[64:128, :]), reads=[rr_], writes=[rr_])
                        mk.op("dve", lambda: nc.vector.tensor_tensor(out=o_[:], in0=po[0:64, :], in1=r_[64:128, :], op=ALU.mult),
                              reads=[por, rr_], writes=[or_])
                        mk.dma("sp", out=m_d[h * 64:(h + 1) * 64, (J - 3) * 128:(J + 1) * 128], in_=o_[:], reads=[or_])
                        oq += 1
        a.close()

    def s5_core(l):
        j = l // 2
        PL = min(S, 1024)
        NPC = S // PL
        I32 = mybir.dt.int32
        TWO_PI = 2.0 * math.pi
        MAGIC = 12582912.0
        a = Arena()
        magic = a.sb("magic", [128, 1], F32)
        lam = a.sb("lam", [128, 16, 3], F32)
        P_ = {n: a.sb("p_" + n, [128, 16], F32) for n in
              ["dt", "lr", "r", "f", "F1", "cs", "sn", "t0", "t1", "t2", "sre", "sim", "nsim"]}
        pi32 = a.sb("pi32", [128, 16], I32)
        pr_ = Res("params")
        lhsB = a.sb("lhsB", [128, 16, 2, 128], BF16)
        lhsC = a.sb("lhsC", [128, 16, 3, 128], BF16)
        lr_ = Res("lhs")
        mk.dma("sp", out=lam[:], in_=s5lam_in[:, j], writes=[pr_])
        mk.op("dve", lambda: nc.vector.memset(magic[:], MAGIC), reads=[pr_], writes=[pr_])

        def po(e, fn):
            mk.op(e, fn, reads=[pr_], writes=[pr_])
        V = nc.vector
        po("act", lambda: nc.scalar.activation(out=P_["dt"][:], in_=lam[:, :, 2], func=AF.Exp))
        po("dve", lambda: V.tensor_tensor(out=P_["lr"][:], in0=lam[:, :, 0], in1=P_["dt"][:], op=ALU.mult))
        po("dve", lambda: V.tensor_tensor(out=P_["f"][:], in0=lam[:, :, 1], in1=P_["dt"][:], op=ALU.mult))
        po("dve", lambda: V.tensor_scalar(out=P_["f"][:], in0=P_["f"][:], scalar1=1.0 / TWO_PI, scalar2=None, op0=ALU.mult))
        po("act", lambda: nc.scalar.activation(out=P_["r"][:], in_=P_["lr"][:], func=AF.Exp))

        def frac(dst, src):
            po("pool", lambda: nc.gpsimd.tensor_copy(out=pi32[:], in_=src))
            po("pool", lambda: nc.gpsimd.tensor_copy(out=P_["t0"][:], in_=pi32[:]))
            po("dve", lambda: V.tensor_tensor(out=dst, in0=src, in1=P_["t0"][:], op=ALU.subtract))
        po("dve", lambda: V.tensor_scalar(out=P_["t1"][:], in0=P_["f"][:], scalar1=64.0, scalar2=None, op0=ALU.mult))
        frac(P_["F1"][:], P_["t1"][:])
        frac(P_["t2"][:], P_["f"][:])
        po("act", lambda: nc.scalar.activation(out=P_["sn"][:], in_=P_["t2"][:], func=AF.Sin, scale=TWO_PI))
        po("act", lambda: nc.scalar.activation(out=P_["t1"][:], in_=P_["t2"][:], func=AF.Abs))
        po("act", lambda: nc.scalar.activation(out=P_["cs"][:], in_=P_["t1"][:], func=AF.Sin, scale=-TWO_PI, bias=math.pi / 2))
        po("dve", lambda: V.tensor_tensor(out=P_["t1"][:], in0=P_["r"][:], in1=P_["cs"][:], op=ALU.mult))
        po("dve", lambda: V.tensor_scalar(out=P_["t1"][:], in0=P_["t1"][:], scalar1=-1.0, scalar2=None, op0=ALU.add))
        po("dve", lambda: V.tensor_tensor(out=P_["t2"][:], in0=P_["r"][:], in1=P_["sn"][:], op=ALU.mult))
        A_, B_ = lam[:, :, 0], lam[:, :, 1]
        po("dve", lambda: V.tensor_tensor(out=P_["sre"][:], in0=P_["t1"][:], in1=A_, op=ALU.mult))
        po("dve", lambda: V.tensor_tensor(out=P_["t0"][:], in0=P_["t2"][:], in1=B_, op=ALU.mult))
        po("dve", lambda: V.tensor_tensor(out=P_["sre"][:], in0=P_["sre"][:], in1=P_["t0"][:], op=ALU.add))
        po("dve", lambda: V.tensor_tensor(out=P_["sim"][:], in0=P_["t2"][:], in1=A_, op=ALU.mult))
        po("dve", lambda: V.tensor_tensor(out=P_["t0"][:], in0=P_["t1"][:], in1=B_, op=ALU.mult))
        po("dve", lambda: V.tensor_tensor(out=P_["sim"][:], in0=P_["sim"][:], in1=P_["t0"][:], op=ALU.subtract))
        po("dve", lambda: V.tensor_tensor(out=P_["t0"][:], in0=A_, in1=A_, op=ALU.mult))
        po("dve", lambda: V.tensor_tensor(out=P_["t1"][:], in0=B_, in1=B_, op=ALU.mult))
        po("dve", lambda: V.tensor_tensor(out=P_["t0"][:], in0=P_["t0"][:], in1=P_["t1"][:], op=ALU.add))
        po("dve", lambda: V.reciprocal(out=P_["t0"][:], in_=P_["t0"][:]))
        po("dve", lambda: V.tensor_tensor(out=P_["sre"][:], in0=P_["sre"][:], in1=P_["t0"][:], op=ALU.mult))
        po("dve", lambda: V.tensor_tensor(out=P_["sim"][:], in0=P_["sim"][:], in1=P_["t0"][:], op=ALU.mult))
        po("dve", lambda: V.tensor_scalar(out=P_["nsim"][:], in0=P_["sim"][:], scalar1=-1.0, scalar2=None, op0=ALU.mult))
        a2 = Arena()
        Bm = a2.sb("Bm", [128, 2, 16, 128], F32)
        Cm = a2.sb("Cm", [128, 2, 16, 128], F32)
        tb1 = a2.sb("tb1", [128, 128], F32)
        tb2 = a2.sb("tb2", [128, 128], F32)
        bmr, cmr, t1r, t2r = Res(), Res(), Res(), Res()
        for ri in range(2):
            mk.dma("sp", out=Bm[:, ri], in_=s5B_in[j, ri], writes=[bmr])
            mk.dma("sp", out=Cm[:, ri], in_=s5C_in[j, ri], writes=[cmr])
        for mc in range(16):
            col = lambda n: P_[n][:, mc:mc + 1]
            mk.op("dve", lambda: V.tensor_scalar(out=tb1[:], in0=Bm[:, 0, mc, :], scalar1=col("sre"), scalar2=None, op0=ALU.mult),
                  reads=[bmr, pr_], writes=[t1r])
            mk.op("dve", lambda: V.scalar_tensor_tensor(out=tb1[:], in0=Bm[:, 1, mc, :], scalar=col("nsim"), in1=tb1[:],
                                                        op0=ALU.mult, op1=ALU.add), reads=[bmr, pr_, t1r], writes=[t1r])
            mk.op("dve", lambda: V.tensor_scalar(out=tb2[:], in0=Bm[:, 1, mc, :], scalar1=col("sre"), scalar2=None, op0=ALU.mult),
                  reads=[bmr, pr_], writes=[t2r])
            mk.op("dve", lambda: V.scalar_tensor_tensor(out=tb2[:], in0=Bm[:, 0, mc, :], scalar=col("sim"), in1=tb2[:],
                                                        op0=ALU.mult, op1=ALU.add), reads=[bmr, pr_, t2r], writes=[t2r])
            for ri, (tt_, tr_) in enumerate([(tb1, t1r), (tb2, t2r)]):
                p, ppr = next_ps()
                mk.op("pe", lambda: nc.tensor.transpose(p[:, 0:128], tt_[:], ident[:]), reads=[tr_, cres], writes=[ppr])
                mk.op("act", lambda: nc.scalar.copy(out=lhsB[:, mc, ri, :], in_=p[:, 0:128]), reads=[ppr], writes=[lr_])
            for ri in range(2):
                p, ppr = next_ps()
                mk.op("pe", lambda: nc.tensor.transpose(p[:, 0:128], Cm[:, ri, mc, :], ident[:]), reads=[cmr, cres], writes=[ppr])
                mk.op("act", lambda: nc.scalar.activation(out=lhsC[:, mc, ri, :], in_=p[:, 0:128], func=AF.Copy,
                                                          scale=(1.0 if ri == 0 else -1.0)), reads=[ppr], writes=[lr_])
                if ri == 0:
                    mk.op("act", lambda: nc.scalar.activation(out=lhsC[:, mc, 2, :], in_=p[:, 0:128], func=AF.Copy, scale=-1.0),
                          reads=[ppr], writes=[lr_])
        a2.close()
        def mkset(tag):
            B = {}
            for nm in ["cosT", "sinT", "vv", "bre", "bim", "gre", "gim"]:
                B[nm] = a.sb(nm + tag, [128, PL], F32)
            B["hre"] = a.sb("hre" + tag, [128, PL], BF16)
            B["him"] = a.sb("him" + tag, [128, PL], BF16)
            B["h3"] = a.sb("h3" + tag, [128, PL], BF16)
            B["h4"] = a.sb("h4" + tag, [128, PL], BF16)
            B["tq"] = [(a.sb("tq%d%s" % (i, tag), [128, TT], F32), Res()) for i in range(2)]
            B["R"] = {n: Res(n) for n in ["cos", "sin", "vv", "bre", "bim", "gre", "gim", "hre", "him", "h3", "h4", "carry"]}
            return B
        SETS = [mkset("A"), mkset("B")]
        yacc = a.sb("yacc", [128, PL], F32)
        uf = a.sb("uf", [128, PL], F32)
        ub = a.sb("ub", [128, PL], BF16)
        carry = a.sb("carry", [128, 16, 2], F32)
        zb = a.sb("zb", [128, PL], F32)
        RS = {n: Res(n) for n in ["iota", "yacc", "uf", "ub", "zb"]}
        NTP = PL // TT

        lhsV = a.sb("lhsV", [2, 16, 128], F32)
        wm = a.sb("wm", [128, 16, 2], F32)
        wmr, lvr = Res("wm"), Res("lhsV")
        mk.op("dve", lambda: V.tensor_copy(out=wm[:, :, 0], in_=P_["F1"][:]), reads=[pr_], writes=[wmr])
        mk.op("dve", lambda: V.tensor_copy(out=wm[:, :, 1], in_=P_["f"][:]), reads=[pr_, wmr], writes=[wmr])
        for mc_ in range(16):
            pt_, ptr_ = next_ps()
            mk.op("pe", lambda: nc.tensor.transpose(pt_[0:2, 0:128], wm[:, mc_, :], ident[:]), reads=[wmr, cres], writes=[ptr_])
            mk.op("act", lambda: nc.scalar.copy(out=lhsV[:, mc_, :], in_=pt_[0:2, 0:128]), reads=[ptr_], writes=[lvr])
        iota2 = [a.sb("iota2_%d" % i, [2, PL], F32) for i in range(NPC)]
        for pc_ in range(NPC):
            mk.dma("sp", out=iota2[pc_][0:1, :], in_=iota_in[0:1, 0, pc_ * PL:(pc_ + 1) * PL], writes=[RS["iota"]])
            mk.dma("sp", out=iota2[pc_][1:2, :], in_=iota_in[0:1, 1, 0:PL], writes=[RS["iota"]])

        def s5_iter(pc, cc, m4, B):
            mc = cc * 4 + m4
            col = lambda n: P_[n][:, mc:mc + 1]
            cosT, sinT, vv, bre, bim, gre, gim, hre, him = (B[n] for n in ["cosT", "sinT", "vv", "bre", "bim", "gre", "gim", "hre", "him"])
            R = B["R"]
            tq = B["tq"]
            for t in range(NTP):
                sl = slice(t * TT, (t + 1) * TT)
                pv, pvr = next_ps()
                mk.op("pe", lambda: nc.tensor.matmul(pv[:], lhsT=lhsV[:, mc, :], rhs=iota2[pc][:, sl], start=True, stop=True),
                      reads=[lvr, RS["iota"]], writes=[pvr])
                mk.op("act", lambda: nc.scalar.activation(out=sinT[:, sl], in_=pv[:], func=AF.Identity, bias=magic[:], scale=1.0),
                      reads=[pvr, R["sin"], pr_], writes=[R["sin"]])
                yield
                mk.op("dve", lambda: V.scalar_tensor_tensor(out=vv[:, sl], in0=sinT[:, sl], scalar=MAGIC, in1=pv[:],
                                                            op0=ALU.subtract, op1=ALU.subtract),
                      reads=[R["sin"], pvr, R["vv"]], writes=[R["vv"]])
                yield
            mk.op("act", lambda: nc.scalar.activation(out=sinT[:], in_=vv[:], func=AF.Sin, scale=-TWO_PI),
                  reads=[R["vv"]], writes=[R["sin"]])
            yield
            mk.op("act", lambda: nc.scalar.activation(out=vv[:], in_=vv[:], func=AF.Abs), reads=[R["vv"]], writes=[R["vv"]])
            yield
            mk.op("act", lambda: nc.scalar.activation(out=cosT[:], in_=vv[:], func=AF.Sin, scale=-TWO_PI, bias=math.pi / 2),
                  reads=[R["vv"]], writes=[R["cos"]])
            yield
            for t in range(NTP):
                sl = slice(t * TT, (t + 1) * TT)
                p1, p1r = next_ps()
                p2, p2r = next_ps()
                mk.op("pe", lambda: nc.tensor.matmul(p1[:], lhsT=lhsB[:, mc, 0, :], rhs=ub[:, sl], start=True, stop=True),
                      reads=[lr_, RS["ub"]], writes=[p1r])
                mk.op("pe", lambda: nc.tensor.matmul(p2[:], lhsT=lhsB[:, mc, 1, :], rhs=ub[:, sl], start=True, stop=True),
                      reads=[lr_, RS["ub"]], writes=[p2r])
                q1, q1r = tq[0]
                q2, q2r = tq[1]
                yield
                mk.op("dve", lambda: V.tensor_tensor(out=bre[:, sl], in0=p1[:], in1=cosT[:, sl], op=ALU.mult),
                      reads=[p1r, R["cos"]], writes=[R["bre"]])
                yield
                mk.op("dve", lambda: V.tensor_tensor(out=q1[:], in0=p2[:], in1=sinT[:, sl], op=ALU.mult),
                      reads=[p2r, R["sin"]], writes=[q1r])
                yield
                mk.op("pool", lambda: nc.gpsimd.tensor_tensor(out=bre[:, sl], in0=bre[:, sl], in1=q1[:], op=ALU.add),
                      reads=[q1r, R["bre"]], writes=[R["bre"]])
                yield
                mk.op("dve", lambda: V.tensor_tensor(out=bim[:, sl], in0=p2[:], in1=cosT[:, sl], op=ALU.mult),
                      reads=[p2r, R["cos"]], writes=[R["bim"]])
                yield
                mk.op("dve", lambda: V.tensor_tensor(out=q2[:], in0=p1[:], in1=sinT[:, sl], op=ALU.mult),
                      reads=[p1r, R["sin"]], writes=[q2r])
                yield
                mk.op("pool", lambda: nc.gpsimd.tensor_tensor(out=bim[:, sl], in0=bim[:, sl], in1=q2[:], op=ALU.subtract),
                      reads=[q2r, R["bim"]], writes=[R["bim"]])
                yield
            ini_re = 0.0 if pc == 0 else carry[:, mc, 0:1]
            ini_im = 0.0 if pc == 0 else carry[:, mc, 1:2]
            rb = P_["r"][:, mc:mc + 1].to_broadcast([128, PL])
            mk.op("dve", lambda: V.tensor_tensor_scan(out=gre[:], data0=rb, data1=bre[:], initial=ini_re,
                                                      op0=ALU.mult, op1=ALU.add),
                  reads=[R["bre"], pr_, R["carry"]], writes=[R["gre"]])
            yield
            mk.op("dve", lambda: V.tensor_tensor_scan(out=gim[:], data0=rb, data1=bim[:], initial=ini_im,
                                                      op0=ALU.mult, op1=ALU.add),
                  reads=[R["bim"], pr_, R["carry"]], writes=[R["gim"]])
            yield
            if pc < NPC - 1:
                mk.op("act", lambda: nc.scalar.copy(out=carry[:, mc, 0:1], in_=gre[:, PL - 1:PL]), reads=[R["gre"]], writes=[R["carry"]])
                mk.op("act", lambda: nc.scalar.copy(out=carry[:, mc, 1:2], in_=gim[:, PL - 1:PL]), reads=[R["gim"]], writes=[R["carry"]])
                yield
            h3, h4 = B["h3"], B["h4"]
            mk.op("dve", lambda: V.tensor_tensor(out=hre[:], in0=gre[:], in1=cosT[:], op=ALU.mult),
                  reads=[R["gre"], R["cos"]], writes=[R["hre"]])
            yield
            mk.op("pool", lambda: nc.gpsimd.tensor_tensor(out=him[:], in0=gim[:], in1=sinT[:], op=ALU.mult),
                  reads=[R["gim"], R["sin"]], writes=[R["him"]])
            yield
            mk.op("dve", lambda: V.tensor_tensor(out=h3[:], in0=gre[:], in1=sinT[:], op=ALU.mult),
                  reads=[R["gre"], R["sin"]], writes=[R["h3"]])
            yield
            mk.op("pool", lambda: nc.gpsimd.tensor_tensor(out=h4[:], in0=gim[:], in1=cosT[:], op=ALU.mult),
                  reads=[R["gim"], R["cos"]], writes=[R["h4"]])
            yield
            for t in range(NTP):
                sl = slice(t * TT, (t + 1) * TT)
                p1, p1r = next_ps()
                mk.op("pe", lambda: nc.tensor.matmul(p1[:], lhsT=lhsC[:, mc, 0, :], rhs=hre[:, sl], start=True, stop=False),
                      reads=[lr_, R["hre"]], writes=[p1r], inc=False)
                mk.op("pe", lambda: nc.tensor.matmul(p1[:], lhsT=lhsC[:, mc, 2, :], rhs=him[:, sl], start=False, stop=False),
                      reads=[lr_, R["him"]], writes=[p1r], inc=False)
                mk.op("pe", lambda: nc.tensor.matmul(p1[:], lhsT=lhsC[:, mc, 1, :], rhs=h3[:, sl], start=False, stop=False),
                      reads=[lr_, R["h3"]], writes=[p1r], inc=False)
                mk.op("pe", lambda: nc.tensor.matmul(p1[:], lhsT=lhsC[:, mc, 1, :], rhs=h4[:, sl], start=False, stop=True),
                      reads=[lr_, R["h4"]], writes=[p1r])
                if m4 == 0:
                    mk.op("act", lambda: nc.scalar.copy(out=yacc[:, sl], in_=p1[:]), reads=[p1r], writes=[RS["yacc"]])
                else:
                    mk.op("dve", lambda: V.tensor_tensor(out=yacc[:, sl], in0=p1[:], in1=yacc[:, sl], op=ALU.add),
                          reads=[p1r, RS["yacc"]], writes=[RS["yacc"]])
                yield

        def chain2(g1, g2):
            for _ in g1:
                yield
            for _ in g2:
                yield

        for pc in range(NPC):
            t0 = pc * PL
            for cc in range(4):
                mk.dma("sp", out=uf[:], in_=zA_d[cc * 128:(cc + 1) * 128, t0:t0 + PL], writes=[RS["uf"]])
                mk.op("act", lambda: nc.scalar.copy(out=ub[:], in_=uf[:]), reads=[RS["uf"]], writes=[RS["ub"]])
                pump(4, ("act",))
                gA = chain2(s5_iter(pc, cc, 0, SETS[0]), s5_iter(pc, cc, 2, SETS[0]))
                gB = chain2(s5_iter(pc, cc, 1, SETS[1]), s5_iter(pc, cc, 3, SETS[1]))
                dA = dB = False
                while not (dA and dB):
                    if not dA:
                        dA = next(gA, "done") == "done"
                    if not dB:
                        dB = next(gB, "done") == "done"
                mk.op("dve", lambda: V.scalar_tensor_tensor(out=yacc[:], in0=uf[:], scalar=s5d[:, j, cc, 0:1], in1=yacc[:],
                                                            op0=ALU.mult, op1=ALU.add),
                      reads=[RS["uf"], RS["yacc"], cres], writes=[RS["yacc"]])
                mk.op("act", lambda: nc.scalar.activation(out=zb[:], in_=yacc[:], func=AF.Gelu_apprx_tanh),
                      reads=[RS["yacc"]], writes=[RS["zb"]])
                mk.dma("sp", out=zA_d[512 + cc * 128:512 + (cc + 1) * 128, t0:t0 + PL], in_=zb[:], reads=[RS["zb"]])
        a.close()
        a = Arena()
        zf = a.sb("zf", [128, 4, TT], F32)
        zbf = a.sb("zbf", [128, 4, TT], BF16)
        gw = a.sb("gw", [128, 4, 4, 128], BF16)
        sgt = [(a.sb("gsg%d" % i, [128, TT], F32), Res()) for i in range(2)]
        ot = [(a.sb("got%d" % i, [128, TT], BF16), Res()) for i in range(2)]
        zfr, zbr, gwr = Res(), Res(), Res()
        mk.dma("sp", out=gw[:].rearrange("p n k c -> p n (k c)"), in_=glu_s[j].rearrange("n p x -> p n x"), writes=[gwr])
        for t in range(NT):
            tsl = slice(t * TT, (t + 1) * TT)
            mk.dma("sp", out=zf[:], in_=zA_d[512:1024, tsl].rearrange("(k p) s -> p k s", p=128), writes=[zfr])
            mk.op("act", lambda: nc.scalar.copy(out=zbf[:], in_=zf[:]), reads=[zfr], writes=[zbr])
            for cc in range(4):
                p, ppr = next_ps()
                for k in range(4):
                    mk.op("pe", lambda: nc.tensor.matmul(p[:], lhsT=gw[:, cc, k, :], rhs=zbf[:, k, :], start=(k == 0), stop=(k == 3)),
                          reads=[gwr, zbr], writes=[ppr], inc=(k == 3))
                sgx, sgr = sgt[cc % 2]
                o_, or_ = ot[cc % 2]
                mk.op("act", lambda: nc.scalar.activation(out=sgx[:], in_=p[:], func=AF.Sigmoid, bias=s5d[:, j, cc, 1:2]),
                      reads=[ppr, cres], writes=[sgr])
                mk.op("dve", lambda: nc.vector.tensor_tensor(out=o_[:], in0=sgx[:], in1=zf[:, cc, :], op=ALU.mult),
                      reads=[sgr, zfr], writes=[or_])
                mk.dma("sp", out=m_d[512 + cc * 128:512 + (cc + 1) * 128, tsl], in_=o_[:], reads=[or_])
        a.close()

    def load_x(src, t):
        mk.dma("sp", out=TL["xT"][:], in_=src[:, t * TT:(t + 1) * TT].rearrange("(k p) s -> p k s", p=128),
               writes=TL["xr"])

    def store_x(dst, t):
        mk.dma("sp", out=dst[:, t * TT:(t + 1) * TT].rearrange("(k p) s -> p k s", p=128), in_=TL["xT"][:],
               reads=TL["xr"])

    def gelu_tanh(a, out, x, xres_, outres, tmp, tmpres, n):
        mk.op("act", lambda: nc.scalar.activation(out=tmp, in_=x, func=AF.Square), reads=[xres_], writes=[tmpres])
        mk.op("dve", lambda: nc.vector.tensor_scalar(out=tmp, in0=tmp, scalar1=0.044715, scalar2=1.0,
                                                     op0=ALU.mult, op1=ALU.add), reads=[tmpres], writes=[tmpres])
        mk.op("dve", lambda: nc.vector.tensor_tensor(out=tmp, in0=tmp, in1=x, op=ALU.mult), reads=[tmpres, xres_], writes=[tmpres])
        mk.op("act", lambda: nc.scalar.activation(out=tmp, in_=tmp, func=AF.Sigmoid, scale=1.5957691216057308),
              reads=[tmpres], writes=[tmpres])
        mk.op("dve", lambda: nc.vector.tensor_tensor(out=out, in0=tmp, in1=x, op=ALU.mult), reads=[tmpres, xres_], writes=[outres])

    def rglru_gen(l, nps, ccs):
        j = l // 2
        HS = S // 2 if S >= 2048 else S
        NH = S // HS
        NTH = HS // TT
        a = Arena()
        xa = a.sb("xa", [128, HS + 3], F32)
        ya = a.sb("ya", [128, HS], F32)
        xc = a.sb("xc", [128, HS], F32)
        xcb = a.sb("xcb", [128, HS], BF16)
        rr = a.sb("rr", [128, HS], F32)
        ii = a.sb("ii", [128, HS], F32)
        t1 = a.sb("t1", [128, HS], F32)
        ob = a.sb("ob", [128, HS], BF16)
        wst = a.sb("wst", [128, 2, 128], F32)
        wbd = a.sb("wbd", [128, 2, 128], BF16)
        sc = a.sb("sc", [128, 4], F32)
        hc = a.sb("hc", [128, 4], F32)
        R = {n: Res(n) for n in ["xa", "ya", "xc", "xcb", "rr", "ii", "t1", "ob", "wst", "wbd", "sc", "hc"]}
        mk.op("act", lambda: nc.scalar.activation(out=sc[:], in_=lrup[:, j, :, 7], func=AF.Exp), reads=[cres], writes=[R["sc"]])
        mk.op("act", lambda: nc.scalar.activation(out=sc[:], in_=sc[:], func=AF.Ln, bias=1.0), reads=[R["sc"]], writes=[R["sc"]])
        mk.op("dve", lambda: nc.vector.tensor_scalar(out=sc[:], in0=sc[:], scalar1=-8.0, scalar2=None, op0=ALU.mult),
              reads=[R["sc"]], writes=[R["sc"]])
        yield
        for cc in ccs:
            prm = lambda i: lrup[:, j, cc, i:i + 1]
            mk.dma("sp", out=wst[:], in_=lru_wbd[j, :, cc].rearrange("a p q -> p a q"), writes=[R["wst"]])
            mk.op("pool", lambda: nc.gpsimd.tensor_copy(out=wbd[:], in_=wst[:]), reads=[R["wst"]], writes=[R["wbd"]])
            yield
            for hf in range(NH):
                h0 = hf * HS
                if hf == 0:
                    mk.op("dve", lambda: nc.vector.memset(xa[:, 0:3], 0.0), writes=[R["xa"]])
                    mk.dma("sp", out=xa[:, 3:3 + HS], in_=zA_d[cc * 128:(cc + 1) * 128, 0:HS], writes=[R["xa"]])
                else:
                    mk.dma("sp", out=xa[:, 0:3 + HS], in_=zA_d[cc * 128:(cc + 1) * 128, h0 - 3:h0 + HS], writes=[R["xa"]])
                mk.dma("sp", out=ya[:], in_=zA_d[512 + cc * 128:512 + (cc + 1) * 128, h0:h0 + HS], writes=[R["ya"]])
                yield
                mk.op("dve", lambda: nc.vector.tensor_scalar(out=xc[:], in0=xa[:, 0:HS], scalar1=prm(0), scalar2=prm(4),
                                                             op0=ALU.mult, op1=ALU.add), reads=[R["xa"], cres], writes=[R["xc"]])
                yield
                for tap in range(1, 4):
                    mk.op("dve", lambda: nc.vector.scalar_tensor_tensor(out=xc[:], in0=xa[:, tap:tap + HS], scalar=prm(tap), in1=xc[:],
                                                                        op0=ALU.mult, op1=ALU.add),
                          reads=[R["xa"], R["xc"], cres], writes=[R["xc"]])
                    yield
                mk.op("act", lambda: nc.scalar.copy(out=xcb[:], in_=xc[:]), reads=[R["xc"]], writes=[R["xcb"]])
                yield
                for t in range(NTH):
                    tsl = slice(t * TT, (t + 1) * TT)
                    p, pr = next_ps(nps)
                    mk.op("pe", lambda: nc.tensor.matmul(p[:], lhsT=wbd[:, 0, :], rhs=xcb[:, tsl], start=True, stop=True),
                          reads=[R["wbd"], R["xcb"]], writes=[pr])
                    mk.op("act", lambda: nc.scalar.activation(out=rr[:, tsl], in_=p[:], func=AF.Sigmoid, bias=prm(5)),
                          reads=[pr, cres], writes=[R["rr"]])
                    yield
                    p, pr = next_ps(nps)
                    mk.op("pe", lambda: nc.tensor.matmul(p[:], lhsT=wbd[:, 1, :], rhs=xcb[:, tsl], start=True, stop=True),
                          reads=[R["wbd"], R["xcb"]], writes=[pr])
                    mk.op("act", lambda: nc.scalar.activation(out=ii[:, tsl], in_=p[:], func=AF.Sigmoid, bias=prm(6)),
                          reads=[pr, cres], writes=[R["ii"]])
                    yield
                mk.op("act", lambda: nc.scalar.activation(out=rr[:], in_=rr[:], func=AF.Exp, scale=sc[:, cc:cc + 1]),
                      reads=[R["rr"], R["sc"]], writes=[R["rr"]])
                yield
                mk.op("act", lambda: nc.scalar.activation(out=t1[:], in_=rr[:], func=AF.Square), reads=[R["rr"]], writes=[R["t1"]])
                yield
                mk.op("act", lambda: nc.scalar.activation(out=t1[:], in_=t1[:], func=AF.Sqrt, scale=-1.0, bias=1.0),
                      reads=[R["t1"]], writes=[R["t1"]])
                yield
                mk.op("dve", lambda: nc.vector.tensor_tensor(out=t1[:], in0=t1[:], in1=ii[:], op=ALU.mult),
                      reads=[R["t1"], R["ii"]], writes=[R["t1"]])
                yield
                mk.op("dve", lambda: nc.vector.tensor_tensor(out=t1[:], in0=t1[:], in1=xc[:], op=ALU.mult),
                      reads=[R["t1"], R["xc"]], writes=[R["t1"]])
                yield
                ini = 0.0 if hf == 0 else hc[:, cc:cc + 1]
                mk.op("dve", lambda: nc.vector.tensor_tensor_scan(out=ii[:], data0=rr[:], data1=t1[:], initial=ini,
                                                                  op0=ALU.mult, op1=ALU.add),
                      reads=[R["rr"], R["t1"], R["ii"], R["hc"]], writes=[R["ii"]])
                yield
                if hf < NH - 1:
                    mk.op("act", lambda: nc.scalar.copy(out=hc[:, cc:cc + 1], in_=ii[:, HS - 1:HS]), reads=[R["ii"]], writes=[R["hc"]])
                mk.op("act", lambda: nc.scalar.activation(out=t1[:], in_=ya[:], func=AF.Gelu_apprx_tanh),
                      reads=[R["ya"], R["t1"]], writes=[R["t1"]])
                yield
                mk.op("dve", lambda: nc.vector.tensor_tensor(out=ob[:], in0=t1[:], in1=ii[:], op=ALU.mult),
                      reads=[R["t1"], R["ii"]], writes=[R["ob"]])
                mk.dma("sp", out=m_d[cc * 128:(cc + 1) * 128, h0:h0 + HS], in_=ob[:], reads=[R["ob"]])
                yield
            pump(4, ("pool",))
            yield
        yield "closing"
        a.close()

    def rglru_core(l):
        ga = rglru_gen(l, 8, [0, 1])
        gb = rglru_gen(l, 8, [2, 3])
        da = db = False
        while not (da and db):
            if not da:
                da = next(ga, "done") in ("done", "closing")
            if not db:
                db = next(gb, "done") in ("done", "closing")
        for g in (gb, ga):
            for _ in g:
                pass

    def fox_core(l, f_sb, f_r):
        j = l // 2
        a = Arena()
        Csb = a.sb("Csb", [8, S], F32)
        onesf = a.sb("onesf", [8, S], F32)
        tmpf = a.sb("tmpf", [8, S], F32)
        cb = [a.sb("cb%d" % i, [8, S], BF16) for i in range(3)]
        nbf = a.sb("nbf", [8, 1], F32)
        Ccol = a.sb("Ccol", [128, NB, 8], F32)
        Qa = [(a.sb("Qa%d" % i, [67, S], BF16), Res()) for i in range(2)]
        Ka = [(a.sb("Ka%d" % i, [67, S], BF16), Res()) for i in range(2)]
        Va = [(a.sb("Va%d" % i, [128, NB, 128], BF16), Res()) for i in range(2)]
        pt = [(a.sb("pt%d" % i, [128, 512], BF16), Res()) for i in range(4)]
        rd = [(a.sb("rd%d" % i, [128, 512], F32), Res()) for i in range(2)]
        obs = [(a.sb("obs%d" % i, [64, 512], BF16), Res()) for i in range(2)]
        R = {n: Res(n) for n in ["C", "ones", "tmp", "cb", "nbf", "Ccol", "csd"]}
        for i in range(2):
            mk.op("dve", lambda: nc.vector.memset(Ka[i][0][64:67, :], 1.0), writes=[Ka[i][1]])
            mk.op("pool", lambda: nc.gpsimd.memset(Va[i][0][:, :, 64:128], 1.0), writes=[Va[i][1]])
        mk.op("pool", lambda: nc.gpsimd.memset(onesf[:], 1.0), writes=[R["ones"]])
        mk.op("dve", lambda: nc.vector.tensor_scalar(out=nbf[:], in0=foxp[0:8, j, 2:3], scalar1=-1.0, scalar2=None, op0=ALU.mult),
              reads=[cres], writes=[R["nbf"]])
        mk.op("act", lambda: nc.scalar.activation(out=f_sb[:], in_=f_sb[:], func=AF.Exp, scale=-1.0, bias=nbf[:]),
              reads=[f_r, R["nbf"]], writes=[f_r])
        mk.op("act", lambda: nc.scalar.activation(out=f_sb[:], in_=f_sb[:], func=AF.Ln, bias=1.0), reads=[f_r], writes=[f_r])
        mk.op("dve", lambda: nc.vector.tensor_tensor_scan(out=Csb[:], data0=onesf[:], data1=f_sb[:], initial=0.0,
                                                          op0=ALU.mult, op1=ALU.add),
              reads=[f_r, R["ones"]], writes=[R["C"]])
        pc, pcr = ps[0], psr[0]
        for J in range(NB):
            mk.op("pe", lambda: nc.tensor.transpose(pc[:, J * 8:(J + 1) * 8], Csb[:, J * 128:(J + 1) * 128], ident[0:8, 0:8]),
                  reads=[R["C"], cres], writes=[pcr], inc=(J == NB - 1))
        mk.op("dve", lambda: nc.vector.tensor_copy(out=Ccol[:].rearrange("p j h -> p (j h)"), in_=pc[:, 0:NB * 8]),
              reads=[pcr], writes=[R["Ccol"]])
        mk.op("dve", lambda: nc.vector.tensor_scalar(out=tmpf[:], in0=Csb[:], scalar1=-8.0, scalar2=None, op0=ALU.mult),
              reads=[R["C"]], writes=[R["tmp"]])
        for i in range(3):
            mk.op("dve", lambda: nc.vector.tensor_copy(out=cb[i][:], in_=tmpf[:]), reads=[R["tmp"]], writes=[R["cb"]])
            if i < 2:
                mk.op("dve", lambda: nc.vector.tensor_tensor(out=tmpf[:], in0=tmpf[:], in1=cb[i][:], op=ALU.subtract),
                      reads=[R["tmp"], R["cb"]], writes=[R["tmp"]])
            mk.dma("sp", out=cs_d[i], in_=cb[i][:], reads=[R["cb"]], writes=[R["csd"]])
        oq = 0
        ptc = [0]
        for h in range(8):
            Q, Qr = Qa[h % 2]
            Kt, Kr = Ka[h % 2]
            V, Vr = Va[h % 2]
            mk.dma("sp", out=Q[0:64, :], in_=qk_d[h * 64:(h + 1) * 64, :], writes=[Qr])
            mk.dma("sp", out=Q[64:67, :], in_=cs_d[:, h, :], reads=[R["csd"]], writes=[Qr])
            mk.dma("sp", out=Kt[0:64, :], in_=qk_d[512 + h * 64:512 + (h + 1) * 64, :], writes=[Kr])
            mk.dma("sp", out=V[:, :, 0:64], in_=v_d[:, h * 64:(h + 1) * 64].rearrange("(j p) d -> p j d", p=128), writes=[Vr])
            items = [(I, J) for I in range(NT) for J in range(4 * I + 4)]

            def qk(I, J):
                b = J - 4 * I
                c0 = b * 128 if b > 0 else 0
                pS, pSr = next_ps(6)
                mk.op("pe", lambda: nc.tensor.matmul(pS[:, c0:512], lhsT=Kt[0:67, J * 128:(J + 1) * 128],
                                                     rhs=Q[0:67, I * 512 + c0:(I + 1) * 512], start=True, stop=True),
                      reads=[Kr, Qr], writes=[pSr])
                return pS, pSr, b, c0
            LA = 3
            qq = [qk(*items[i0]) for i0 in range(min(LA, len(items)))]
            for ii_, (I, J) in enumerate(items):
                nJ = 4 * I + 4
                po, por = ps[6 + (oq % 2)], psr[6 + (oq % 2)]
                pS, pSr, b, c0 = qq.pop(0)
                if ii_ + LA < len(items):
                    qq.append(qk(*items[ii_ + LA]))
                if b >= 0:
                    mk.op("dve", lambda: nc.vector.tensor_tensor(out=pS[:, c0:c0 + 128], in0=pS[:, c0:c0 + 128],
                                                                 in1=negmask[:], op=ALU.add),
                          reads=[pSr, cres], writes=[pSr])
                P, Pr = pt[ptc[0] % 4]
                ptc[0] += 1
                mk.op("act", lambda: nc.scalar.activation(out=P[:, c0:512], in_=pS[:, c0:512], func=AF.Exp,
                                                          scale=0.125, bias=Ccol[:, J, h:h + 1]),
                      reads=[pSr, R["Ccol"]], writes=[Pr])
                mk.op("pe", lambda: nc.tensor.matmul(po[:, c0:512], lhsT=V[:, J, :], rhs=P[:, c0:512],
                                                     start=(J == 0), stop=(J == nJ - 1)),
                      reads=[Vr, Pr], writes=[por], inc=True)
                if J < nJ - 1:
                    continue
                r_, rr_ = rd[oq % 2]
                o_, or_ = obs[oq % 2]
                mk.op("dve", lambda: nc.vector.reciprocal(out=r_[64:128, :], in_=po[64:128, :]), reads=[por], writes=[rr_])
                mk.op("dve", lambda: nc.vector.tensor_tensor(out=o_[:], in0=po[0:64, :], in1=r_[64:128, :], op=ALU.mult),
                      reads=[por, rr_], writes=[or_])
                mk.dma("sp", out=m_d[512 + h * 64:512 + (h + 1) * 64, I * 512:(I + 1) * 512], in_=o_[:], reads=[or_])
                oq += 1
                pump(1, ("dve", "pool"))
        a.close()

    def run_stage(kind, l, src, dst, f_sb=None, f_r=None, which=0, with_out=False):
        a = Arena()
        tl_alloc(a, kind)

        def pre(t):
            sel(t)
            if kind == "ffn":
                if with_out:
                    outproj(l, t)
                ffn(l, which, "pre")
            elif kind == "ple":
                ple(l, t, "pre")
            else:
                rmsnorm_x(1, l)

        def loadx(t):
            sel(t)
            load_x(src, t)

        def storex(t):
            sel(t)
            store_x(dst, t)

        def main(t):
            def hook():
                if dst is not None and t >= 1:
                    storex(t - 1)
                if t + 2 < NT:
                    loadx(t + 2)
                sel(t)
            sel(t)
            TL["hook"] = hook
            if kind == "ffn":
                ffn(l, which, "main")
            elif kind == "ple":
                ple(l, t, "main")
            else:
                run_hook()
                if l % 2 == 0:
                    inproj_even(l, t, f_sb, f_r)
                else:
                    inproj_odd(l, t)
            run_hook()

        loadx(0)
        if NT > 1:
            loadx(1)
        pre(0)
        for t in range(NT):
            if t + 1 < NT:
                pre(t + 1)
            main(t)
        if dst is not None:
            storex(NT - 1)
        a.close()

    for l in range(L + 1):
        la = Arena()
        f_sb = la.sb("f_sb", [8, S], F32) if (mixers and l % 2 == 0 and l < L) else None
        f_r = Res("f")
        src = xT_in if l == 0 else xres_d
        if l > 0:
            run_stage("ffn", l - 1, src, xres_d, which=1, with_out=mixers)
            run_stage("ple", l - 1, xres_d, yT_out if l == L else xres_d)
            src = xres_d
        if l < L:
            run_stage("ffn", l, src, xres_d, which=0)
            if mixers:
                run_stage("inproj", l, xres_d, None, f_sb=f_sb, f_r=f_r)
        if l < L:
            enqueue_second_half(l)
            if l + 1 < L:
                enqueue_first_half(l + 1)
        if l < L and mixers:
            if l % 2 == 0:
                rglru_core(l)
                fox_core(l, f_sb, f_r)
            else:
                swa_core(l)
                s5_core(l)
        pump(10 ** 6)
        la.close()
    mk.barrier()
    glob.es.close()
    es.close()
    return nc


def host_prep(inputs, S=4096, L=4, mixers=True):
    f32 = np.float32
    L2 = (L + 1) // 2
    maps = []
    g = np.stack([inputs[n][:L] for n in ["ffn1_norm", "mix_norm", "ffn2_norm", "ple_norm", "ple_gate_norm"]], 0)
    gains = np.ascontiguousarray(g.reshape(5, L, 8, 128).transpose(3, 0, 1, 2)).astype(f32)
    shared = {"gains": gains}
    for nm in ["ffn1_wg", "ffn1_wu", "ffn2_wg", "ffn2_wu", "ffn1_wd", "ffn2_wd", "ple_w", "ple_gate_w"]:
        shared[nm] = np.ascontiguousarray(inputs[nm][:L])
    shared["ident"] = np.eye(128, dtype=f32)
    jj = np.arange(128)[:, None]
    ii = np.arange(128)[None, :]
    shared["negmask"] = np.where(jj <= ii, 0.0, -240000.0).astype(f32)
    if mixers:
        shared["ev_w_in"] = np.ascontiguousarray(inputs["ev_w_in"][:L2])
        shared["ev_w_out"] = np.ascontiguousarray(inputs["ev_w_out"][:L2])
        cols = [inputs["lru_conv_w"][:L2, t] for t in range(4)] + [inputs[n][:L2] for n in
                                                                  ["lru_conv_b", "lru_ba", "lru_bx", "lru_lambda"]]
        lp = np.stack(cols, -1)
        shared["lrup"] = np.ascontiguousarray(lp.reshape(L2, 4, 128, 8).transpose(2, 0, 1, 3)).astype(f32)
        wbd = np.zeros((L2, 2, 4, 128, 128), f32)
        for a, nm in enumerate(["lru_wa", "lru_wx"]):
            w = inputs[nm][:L2]
            for cc in range(4):
                wbd[:, a, cc, 0:64, 0:64] = w[:, 2 * cc]
                wbd[:, a, cc, 64:128, 64:128] = w[:, 2 * cc + 1]
        shared["lru_wbd"] = wbd
        fp = np.zeros((128, L2, 3), f32)
        fp[:, :, 0] = np.tile(inputs["fox_q_norm"][:L2], (1, 2)).T
        fp[:, :, 1] = np.tile(inputs["fox_k_norm"][:L2], (1, 2)).T
        fp[0:8, :, 2] = inputs["fox_bf"][:L2].T
        shared["foxp"] = fp
        LO = max(1, L // 2)
        shared["od_w_in"] = np.ascontiguousarray(inputs["od_w_in"][:LO])
        shared["od_w_out"] = np.ascontiguousarray(inputs["od_w_out"][:LO])
        sw = np.zeros((128, LO, 12), f32)
        for ci, nm in [(0, "swa_q_norm"), (2, "swa_k_norm")]:
            gq = inputs[nm][:LO]
            gsw = np.concatenate([gq[:, 32:], gq[:, :32]], 1)
            sw[:, :, ci] = np.tile(gq, (1, 2)).T
            sw[:, :, ci + 1] = np.tile(gsw, (1, 2)).T
        sw[:, :, 4:12] = inputs["swa_sinks"][:LO][None, :, :]
        shared["swap"] = sw
        half = 32
        inv = np.power(np.float32(10000.0), -np.arange(half, dtype=f32) / np.float32(half)).astype(f32)
        ang = (np.arange(S, dtype=f32)[None, :] * inv[:, None]).astype(f32)
        cosv = np.cos(ang.astype(np.float64)).astype(f32)
        sinv = np.sin(ang.astype(np.float64)).astype(f32)
        shared["ropec"] = np.ascontiguousarray(np.concatenate([cosv, cosv, cosv, cosv], 0))
        shared["ropes"] = np.ascontiguousarray(np.concatenate([-sinv, sinv, -sinv, sinv], 0))
        jj2 = np.arange(128)[:, None]
        ii2 = np.arange(256)[None, :]
        shared["bandmask"] = np.where(ii2 < 128, jj2 <= ii2, jj2 > ii2 - 128).astype(f32)
        tt = np.arange(S)
        shared["iota_ab"] = np.ascontiguousarray(np.broadcast_to(
            np.stack([tt // 64, tt % 64], 0).astype(f32)[None], (128, 2, S)))
        lamr = inputs["s5_lambda_re"][:LO].reshape(LO, 16, 2, 64)
        lami = inputs["s5_lambda_im"][:LO].reshape(LO, 16, 2, 64)
        ldt = np.broadcast_to(inputs["s5_log_dt"][:LO].reshape(LO, 16, 2, 1), (LO, 16, 2, 64))
        sl = np.stack([lamr, lami, ldt], -1)
        shared["s5lam"] = np.ascontiguousarray(sl.transpose(2, 3, 0, 1, 4).reshape(128, LO, 16, 3)).astype(f32)
        sB = np.zeros((LO, 2, 128, 16, 128), f32)
        sC = np.zeros((LO, 2, 128, 16, 128), f32)
        for ri, (bn, cn) in enumerate([("s5_b_re", "s5_c_re"), ("s5_b_im", "s5_c_im")]):
            bsrc = inputs[bn][:LO]
            csrc = inputs[cn][:LO]
            for g in range(32):
                sB[:, ri, (g % 2) * 64:(g % 2) * 64 + 64, g // 2, (g % 8) * 16:(g % 8) * 16 + 16] = bsrc[:, g]
                sC[:, ri, (g % 8) * 16:(g % 8) * 16 + 16, g // 2, (g % 2) * 64:(g % 2) * 64 + 64] = csrc[:, g]
        shared["s5B"] = sB
        shared["s5C"] = sC
        sd = np.stack([inputs["s5_d"][:LO], inputs["s5_glu_b"][:LO]], -1)
        shared["s5d"] = np.ascontiguousarray(sd.reshape(LO, 4, 128, 2).transpose(2, 0, 1, 3)).astype(f32)
        shared["s5_glu_w"] = np.ascontiguousarray(inputs["s5_glu_w"][:LO])
    B = inputs["x"].shape[0]
    for b in range(B):
        m = dict(shared)
        m["xT"] = np.ascontiguousarray(inputs["x"][b, :S].T)
        m["pT"] = np.ascontiguousarray(inputs["p"][:L, b, :S].transpose(0, 2, 1))
        maps.append(m)
    return maps


def kernel(**inputs):
    inputs = {k: np.asarray(v) for k, v in inputs.items()}
    nc = build()
    maps = host_prep(inputs)
    res = run_bass_kernel_spmd(nc, maps, core_ids=list(range(8)))
    out = np.stack([np.ascontiguousarray(r["yT"].T) for r in res.results], 0)
    return out.astype(np.float32)
```

```python
import math
from contextlib import ExitStack
import numpy as np
import concourse.bass as bass
import concourse.mybir as mybir
from concourse.bass_utils import run_bass_kernel_spmd

F32 = mybir.dt.float32
BF16 = mybir.dt.bfloat16
AF = mybir.ActivationFunctionType
ALU = mybir.AluOpType

D = 1024
FF = 2816
NFC = FF // 128
TT = 512
EPS = 1e-6
NSLOT = 8
EV_IN = 2568
OD_IN = 1280


class Res:
    __slots__ = ("w", "r", "name")

    def __init__(self, name=""):
        self.w = None
        self.r = {}
        self.name = name


class MK:
    def __init__(self, nc, es):
        self.nc = nc
        self.E = {"pe": nc.tensor, "act": nc.scalar, "dve": nc.vector, "pool": nc.gpsimd, "sp": nc.sync}
        self.csem = {e: es.enter_context(nc.semaphore("c_" + e)) for e in ["pe", "act", "dve", "pool"]}
        self.cnt = {e: 0 for e in self.csem}
        self.seen = {e: {} for e in self.E}
        self.dq = {q: [[es.enter_context(nc.semaphore("d_%s%d" % (q, i))), 0] for i in range(NSLOT)]
                   for q in ["sp", "act"]}
        self.dqi = {"sp": 0, "act": 0}
        self.nwait = 0

    def _wait(self, e, ev):
        sem, val = ev
        k = id(sem)
        if self.seen[e].get(k, 0) < val:
            self.E[e].wait_ge(sem, val)
            self.seen[e][k] = val
            self.nwait += 1

    def _deps(self, e, reads, writes):
        own = self.csem.get(e)
        for r in reads:
            if r.w is not None and not (e == "pe" and r.w[0] is own):
                self._wait(e, r.w)
        for w in writes:
            if w.w is not None and not (e == "pe" and w.w[0] is own):
                self._wait(e, w.w)
            for ev in w.r.values():
                if not (e == "pe" and ev[0] is own):
                    self._wait(e, ev)

    def _mark(self, ev, reads, writes):
        k = id(ev[0])
        for r in reads:
            old = r.r.get(k)
            if old is None or old[1] < ev[1]:
                r.r[k] = ev
        for w in writes:
            w.w = ev
            w.r = {}

    def op(self, e, fn, reads=(), writes=(), inc=True):
        self._deps(e, reads, writes)
        inst = fn()
        if inc:
            self.cnt[e] += 1
            inst.then_inc(self.csem[e], 1)
            ev = (self.csem[e], self.cnt[e])
        else:
            ev = (self.csem[e], self.cnt[e] + 1)
        self._mark(ev, reads, writes)
        return inst

    def barrier(self):
        for e in self.E:
            for o in self.csem:
                if o != e and self.cnt[o] > 0:
                    self._wait(e, (self.csem[o], self.cnt[o]))
            for q in self.dq:
                for slot in self.dq[q]:
                    if slot[1] > 0:
                        self._wait(e, (slot[0], slot[1]))

    def dma(self, q, out, in_, reads=(), writes=(), **kw):
        self._deps(q, reads, writes)
        slot = self.dq[q][self.dqi[q] % NSLOT]
        self.dqi[q] += 1
        if slot[1] > 0:
            self._wait(q, (slot[0], slot[1]))
        inst = self.E[q].dma_start(out=out, in_=in_, **kw)
        slot[1] += 16
        inst.then_inc(slot[0], 16)
        ev = (slot[0], slot[1])
        self._mark(ev, reads, writes)
        return ev


def build(S=4096, L=4, mixers=True, dbg=None):
    NT = S // TT
    NB = S // 128
    L2 = (L + 1) // 2
    nc = bass.Bass("TRN2", target_bir_lowering=False)
    es = ExitStack()
    mk = MK(nc, es)
    uid = [0]

    def din(name, shape, dt=F32):
        return nc.dram_tensor(name, list(shape), dt, kind="ExternalInput").ap()

    def dscr(name, shape, dt):
        return nc.dram_tensor(name, list(shape), dt, kind="Internal").ap()

    class Arena:
        def __init__(self):
            self.es = ExitStack()

        def sb(self, name, shape, dt):
            uid[0] += 1
            return self.es.enter_context(nc.sbuf_tensor("%s_%d" % (name, uid[0]), list(shape), dt))

        def close(self):
            mk.barrier()
            self.es.close()

    glob = Arena()

    xT_in = din("xT", [D, S])
    pT_in = din("pT", [L, 256, S])
    gains = din("gains", [128, 5, L, 8])
    w_ffn = {}
    for nm in ["ffn1_wg", "ffn1_wu", "ffn2_wg", "ffn2_wu"]:
        w_ffn[nm] = din(nm, [L, D, FF])
    for nm in ["ffn1_wd", "ffn2_wd"]:
        w_ffn[nm] = din(nm, [L, FF, D])
    ple_w = din("ple_w", [L, 256, D])
    ple_gate_w = din("ple_gate_w", [L, D, D])
    ident_in = din("ident", [128, 128])
    negmask_in = din("negmask", [128, 128])
    if mixers:
        ev_w_in = din("ev_w_in", [L2, D, EV_IN])
        ev_w_out = din("ev_w_out", [L2, D, D])
        lrup_in = din("lrup", [128, L2, 4, 8])
        lru_wbd = din("lru_wbd", [L2, 2, 4, 128, 128])
        foxp_in = din("foxp", [128, L2, 3])
        LO = max(1, L // 2)
        od_w_in = din("od_w_in", [LO, D, OD_IN])
        od_w_out = din("od_w_out", [LO, D, D])
        swap_in = din("swap", [128, LO, 12])
        ropec_in = din("ropec", [128, S])
        ropes_in = din("ropes", [128, S])
        bandmask_in = din("bandmask", [128, 256])
        iota_in = din("iota_ab", [128, 2, S])
        s5lam_in = din("s5lam", [128, LO, 16, 3])
        s5B_in = din("s5B", [LO, 2, 128, 16, 128])
        s5C_in = din("s5C", [LO, 2, 128, 16, 128])
        s5d_in = din("s5d", [128, LO, 4, 2])
        glu_w_in = din("s5_glu_w", [LO, 512, 512])
    yT_out = nc.dram_tensor("yT", [D, S], F32, kind="ExternalOutput").ap()

    xres_d = dscr("xres", [D, S], F32)
    wgu_s = dscr("wgu_s", [L, 4, NFC, 128, 8 * 128], BF16)
    wd_s = dscr("wd_s", [L, 2, 128, NFC * D], BF16)
    plew_s = dscr("plew_s", [L, 8, 128, 2 * 128], BF16)
    pgw_s = dscr("pgw_s", [L, 8, 128, 8 * 128], BF16)
    if mixers:
        evA_s = dscr("evA_s", [L2, 8, 128, 1024], BF16)
        evQK_s = dscr("evQK_s", [L2, 8, 128, 1024], BF16)
        evV_s = dscr("evV_s", [L2, 1, 128, 4096], BF16)
        evF_s = dscr("evF_s", [L2, 1, 128, 64], BF16)
        wout_s = dscr("wout_s", [L, 8, 128, 1024], BF16)
        zA_d = dscr("zA_d", [D, S], F32)
        qk_d = dscr("qk_d", [D, S], BF16)
        v_d = dscr("v_d", [S, 512], BF16)
        cs_d = dscr("cs_d", [3, 8, S], BF16)
        m_d = dscr("m_d", [D, S], BF16)
        od_s = dscr("od_s", [LO, 15, 128, 1024], BF16)
        glu_s = dscr("glu_s", [LO, 4, 128, 512], BF16)
    wres = Res("wscr")

    ones_bf = glob.sb("ones_bf", [128, 128], BF16)
    bd_ones = glob.sb("bd_ones", [128, 128], BF16)
    gains_sb = glob.sb("gains_sb", [128, 5, L, 8], F32)
    eps_col = glob.sb("eps_col", [128, 1], F32)
    ident = glob.sb("ident", [128, 128], F32)
    negmask = glob.sb("negmask", [128, 128], F32)
    cres = Res("consts")
    mk.op("dve", lambda: nc.vector.memset(ones_bf[:], 1.0), writes=[cres])
    mk.op("dve", lambda: nc.vector.memset(bd_ones[:], 0.0), writes=[cres])
    mk.op("dve", lambda: nc.vector.memset(bd_ones[0:64, 0:64], 1.0), writes=[cres])
    mk.op("dve", lambda: nc.vector.memset(bd_ones[64:128, 64:128], 1.0), writes=[cres])
    mk.op("dve", lambda: nc.vector.memset(eps_col[:], EPS), writes=[cres])
    mk.dma("sp", out=gains_sb[:], in_=gains, writes=[cres])
    mk.dma("sp", out=ident[:], in_=ident_in, writes=[cres])
    mk.dma("sp", out=negmask[:], in_=negmask_in, writes=[cres])
    if mixers:
        lrup = glob.sb("lrup", [128, L2, 4, 8], F32)
        foxp = glob.sb("foxp", [128, L2, 3], F32)
        mk.dma("sp", out=lrup[:], in_=lrup_in, writes=[cres])
        mk.dma("sp", out=foxp[:], in_=foxp_in, writes=[cres])
        swap = glob.sb("swap", [128, LO, 12], F32)
        mk.dma("sp", out=swap[:], in_=swap_in, writes=[cres])
        s5d = glob.sb("s5d", [128, LO, 4, 2], F32)
        mk.dma("sp", out=s5d[:], in_=s5d_in, writes=[cres])

    ps = [es.enter_context(nc.psum_tensor("ps%d" % i, [128, 512], F32)) for i in range(8)]
    psr = [Res("ps%d" % i) for i in range(8)]
    psi = [0]

    def next_ps(n=8):
        i = psi[0] % n
        psi[0] += 1
        return ps[i], psr[i]

    stg = [(glob.sb("stg%d" % i, [128, 4096], F32), Res()) for i in range(2)]
    stb = [(glob.sb("stb%d" % i, [128, 4096], BF16), Res()) for i in range(2)]
    pp = [0]
    pending = []

    def pump(n, engs=("dve", "act", "pool")):
        for _ in range(n):
            if not pending:
                return
            blk = pending.pop(0)
            blk(engs[pp[0] % len(engs)])

    def prep(src, K, N, cw, dst, swp=False):
        nk = K // 128
        nn = N // cw
        if nk * cw <= 4096:
            kb = nk
            nch = max(1, min(nn, 4096 // (nk * cw)))
        else:
            nch = 1
            kb = 4096 // cw
        srcv = src.rearrange("(k p) n -> p k n", p=128)
        for n0 in range(0, nn, nch):
            nb = min(nch, nn - n0)
            for k0 in range(0, nk, kb):
                kk = min(kb, nk - k0)
                pending.append(lambda e, n0=n0, nb=nb, k0=k0, kk=kk: prep_block(e, srcv, dst, cw, swp, n0, nb, k0, kk))

    def prep_block(e, srcv, dst, cw, swp, n0, nb, k0, kk):
        i = pp[0] % 2
        pp[0] += 1
        st, sr = stg[i]
        bt, br = stb[i]
        ne = kk * nb * cw
        stv = st[:, 0:ne].rearrange("p (k n) -> p k n", k=kk)
        if not swp:
            mk.dma("sp", out=stv, in_=srcv[:, k0:k0 + kk, n0 * cw:(n0 + nb) * cw], writes=[sr])
        else:
            sv4 = stv.rearrange("p k (g two c) -> p k g two c", two=2, c=32)
            iv4 = srcv[:, k0:k0 + kk, n0 * cw:(n0 + nb) * cw].rearrange("p k (g two c) -> p k g two c", two=2, c=32)
            for kx in range(kk):
                mk.dma("sp", out=sv4[:, kx, :, 0, :], in_=iv4[:, kx, :, 1, :], writes=[sr])
                mk.dma("sp", out=sv4[:, kx, :, 1, :], in_=iv4[:, kx, :, 0, :], writes=[sr])
        inv = st[:, 0:ne].rearrange("p (k n c) -> p n k c", k=kk, n=nb)
        outv = bt[:, 0:ne].rearrange("p (n k c) -> p n k c", n=nb, k=kk)
        if e == "act":
            mk.op("act", lambda: nc.scalar.copy(out=outv, in_=inv), reads=[sr], writes=[br])
        elif e == "dve":
            mk.op("dve", lambda: nc.vector.tensor_copy(out=outv, in_=inv), reads=[sr], writes=[br])
        else:
            mk.op("pool", lambda: nc.gpsimd.tensor_copy(out=outv, in_=inv), reads=[sr], writes=[br])
        dv = dst[n0:n0 + nb, :, k0 * cw:(k0 + kk) * cw].rearrange("n p x -> p n x")
        mk.dma("sp", out=dv, in_=bt[:, 0:ne].rearrange("p (n x) -> p n x", n=nb), reads=[br], writes=[wres])

    def enqueue_first_half(l):
        for j, nm in enumerate(["ffn1_wg", "ffn1_wu"]):
            prep(w_ffn[nm][l], D, FF, 128, wgu_s[l, j])
        prep(w_ffn["ffn1_wd"][l], FF, D, D, wd_s[l, 0:1])
        if mixers:
            j = l // 2
            if l % 2 == 0:
                prep(ev_w_in[j][:, 0:1024], D, 1024, 128, evA_s[j])
                prep(ev_w_in[j][:, 1024:2048], D, 1024, 128, evQK_s[j])
                prep(ev_w_in[j][:, 2048:2560], D, 512, 512, evV_s[j])
                prep(ev_w_in[j][:, 2560:2568], D, 8, 8, evF_s[j])
            else:
                prep(od_w_in[j][:, 0:512], D, 512, 128, od_s[j, 0:4])
                prep(od_w_in[j][:, 0:512], D, 512, 128, od_s[j, 4:8], swp=True)
                prep(od_w_in[j][:, 512:640], D, 128, 128, od_s[j, 8:9])
                prep(od_w_in[j][:, 512:640], D, 128, 128, od_s[j, 9:10], swp=True)
                prep(od_w_in[j][:, 640:768], D, 128, 128, od_s[j, 10:11])
                prep(od_w_in[j][:, 768:1280], D, 512, 128, od_s[j, 11:15])
                prep(glu_w_in[j], 512, 512, 128, glu_s[j])

    def enqueue_second_half(l):
        if mixers:
            j = l // 2
            prep((ev_w_out if l % 2 == 0 else od_w_out)[j], D, D, 128, wout_s[l])
        for j, nm in enumerate(["ffn2_wg", "ffn2_wu"]):
            prep(w_ffn[nm][l], D, FF, 128, wgu_s[l, 2 + j])
        prep(w_ffn["ffn2_wd"][l], FF, D, D, wd_s[l, 1:2])
        prep(ple_w[l], 256, D, 128, plew_s[l])
        prep(ple_gate_w[l], D, D, 128, pgw_s[l])

    enqueue_first_half(0)
    pump(10 ** 6)

    TL = {}

    def tl_alloc(a, kind):
        TL.clear()
        sets = []
        for b in range(2):
            d = {}
            d["hT"] = a.sb("hT%d" % b, [128, 8, TT], BF16)
            d["sq"] = a.sb("sq%d" % b, [128, 8, TT], BF16)
            d["rs"] = a.sb("rs%d" % b, [128, TT], F32)
            for nm in ["hr", "sqr", "rsr", "mr"]:
                d[nm] = Res(nm)
            if kind == "ple":
                d["pTf"] = a.sb("pTf%d" % b, [128, 2, TT], F32)
                d["pTb"] = a.sb("pTb%d" % b, [128, 2, TT], BF16)
                d["eT"] = a.sb("eT%d" % b, [128, 8, TT], F32)
                d["pTfr"] = Res()
                d["pTbr"] = Res()
                d["er"] = [Res("e%d" % k) for k in range(8)]
            sets.append(d)
        TL["sets"] = sets
        TL["XT"] = [(a.sb("xT_sb%d" % b, [128, 8, TT], F32), [Res("x%d" % k) for k in range(8)]) for b in range(3)]
        TL["hook"] = None
        TL["sg"] = [(a.sb("sg%d" % i, [128, TT], F32), Res()) for i in range(2)]
        TL["wgb"] = [(a.sb("wgb%d" % i, [128, 8, 128], BF16), Res()) for i in range(3)]
        if kind == "ffn":
            TL["actT"] = a.sb("actT", [128, NFC, TT], BF16)
            TL["actr"] = [Res("act%d" % i) for i in range(NFC)]
            TL["wub"] = [(a.sb("wub%d" % i, [128, 8, 128], BF16), Res()) for i in range(3)]
            TL["wdb"] = [(a.sb("wdb%d" % i, [128, D], BF16), Res()) for i in range(3)]
        if kind == "ple":
            TL["plewb"] = [(a.sb("plewb%d" % i, [128, 2, 128], BF16), Res()) for i in range(2)]
        if kind == "inproj":
            TL["wub"] = [(a.sb("wub%d" % i, [128, 8, 128], BF16), Res()) for i in range(3)]
            TL["zst"] = [(a.sb("zst%d" % i, [128, TT], F32), Res()) for i in range(2)]
            TL["qst"] = [(a.sb("qst%d" % i, [128, TT], BF16), Res()) for i in range(4)]
            TL["sqb4"] = [(a.sb("sqb%d" % i, [128, TT], BF16), Res()) for i in range(4)]
            TL["rs24"] = [(a.sb("rs2%d" % i, [128, TT], F32), Res()) for i in range(4)]
            TL["qa4"] = [(a.sb("qa%d" % i, [128, TT], F32), Res()) for i in range(2)]
            TL["wv"] = a.sb("wv", [128, 8, 512], BF16)
            TL["wf"] = a.sb("wf", [128, 8, 8], BF16)
            TL["ropeT"] = a.sb("ropeT", [128, 2, TT], F32)
            TL["ropeTr"] = Res("ropeT")
            TL["wv2"] = a.sb("wv2", [128, 8, 128], BF16)
            TL["wv2r"] = Res("wv2")
            TL["wvr"] = Res("wv")
            TL["wfr"] = Res("wf")
        TL["wctr"] = 0

    def sel(b):
        TL.update(TL["sets"][b % 2])
        TL["xT"], TL["xr"] = TL["XT"][b % 3]

    def run_hook():
        h = TL.get("hook")
        if h is not None:
            TL["hook"] = None
            h()

    def wslot(kind):
        i = TL["wctr"] % 3
        TL["wctr"] += 1
        return TL[kind][i]

    def rstd_from(src_reads, src_ap, scale):
        sq, sqr, rs, rsr = TL["sq"], TL["sqr"], TL["rs"], TL["rsr"]
        mk.op("act", lambda: nc.scalar.activation(out=sq[:], in_=src_ap, func=AF.Square),
              reads=src_reads, writes=[sqr])
        p, pr = next_ps()
        for k in range(8):
            mk.op("pe", lambda: nc.tensor.matmul(p[:], lhsT=ones_bf[:], rhs=sq[:, k, :], start=(k == 0), stop=(k == 7)),
                  reads=[sqr, cres], writes=[pr], inc=(k == 7))
        mk.op("act", lambda: nc.scalar.activation(out=rs[:], in_=p[:], func=AF.Ln, scale=scale, bias=eps_col[:]),
              reads=[pr, cres], writes=[rsr])
        mk.op("act", lambda: nc.scalar.activation(out=rs[:], in_=rs[:], func=AF.Exp, scale=-0.5), reads=[rsr], writes=[rsr])

    def rmsnorm_x(gi, l):
        xT, xr, hT, hr, rs, rsr = TL["xT"], TL["xr"], TL["hT"], TL["hr"], TL["rs"], TL["rsr"]
        rstd_from(xr, xT[:], 1.0 / D)
        for k in range(8):
            mk.op("dve", lambda: nc.vector.scalar_tensor_tensor(
                out=hT[:, k, :], in0=xT[:, k, :], scalar=gains_sb[:, gi, l, k:k + 1], in1=rs[:],
                op0=ALU.mult, op1=ALU.mult), reads=[xr[k], rsr, cres], writes=[hr])

    def mm8(p, pr, w, wr_, rhs_of_k, rhs_reads, nk=8):
        for k in range(nk):
            mk.op("pe", lambda: nc.tensor.matmul(p, lhsT=w[:, k, :], rhs=rhs_of_k(k), start=(k == 0), stop=(k == nk - 1)),
                  reads=[wr_] + rhs_reads, writes=[pr], inc=(k == nk - 1))

    def ffn(l, which, part):
        gi = 0 if which == 0 else 2
        if part == "pre":
            rmsnorm_x(gi, l)
            return
        xT, xr, hT, hr, actT, actr = TL["xT"], TL["xr"], TL["hT"], TL["hr"], TL["actT"], TL["actr"]
        for fc in range(NFC):
            i = TL["wctr"] % 3
            TL["wctr"] += 1
            wg, wgr = TL["wgb"][i]
            wu, wur = TL["wub"][i]
            mk.dma("sp", out=wg[:].rearrange("p k j -> p (k j)"), in_=wgu_s[l, 2 * which, fc], reads=[wres], writes=[wgr])
            mk.dma("sp", out=wu[:].rearrange("p k j -> p (k j)"), in_=wgu_s[l, 2 * which + 1, fc], reads=[wres], writes=[wur])
            pg, pgr = next_ps()
            pu, pur = next_ps()
            mm8(pg[:], pgr, wg, wgr, lambda k: hT[:, k, :], [hr])
            mm8(pu[:], pur, wu, wur, lambda k: hT[:, k, :], [hr])
            s, sr = TL["sg"][fc % 2]
            mk.op("act", lambda: nc.scalar.activation(out=s[:], in_=pg[:], func=AF.Silu), reads=[pgr], writes=[sr])
            mk.op("dve", lambda: nc.vector.tensor_tensor(out=actT[:, fc, :], in0=pu[:], in1=s[:], op=ALU.mult),
                  reads=[pur, sr], writes=[actr[fc]])
            if fc == 1:
                run_hook()
        for fc in range(NFC):
            wd, wdr = wslot("wdb")
            mk.dma("sp", out=wd[:], in_=wd_s[l, which, :, fc * D:(fc + 1) * D], reads=[wres], writes=[wdr])
            for dc in range(8):
                mk.op("pe", lambda: nc.tensor.matmul(ps[dc][:], lhsT=wd[:, dc * 128:(dc + 1) * 128], rhs=actT[:, fc, :],
                                                     start=(fc == 0), stop=(fc == NFC - 1)),
                      reads=[wdr, actr[fc]], writes=[psr[dc]], inc=(dc == 7 or fc == NFC - 1))
        for dc in range(8):
            mk.op("dve", lambda: nc.vector.scalar_tensor_tensor(
                out=xT[:, dc, :], in0=ps[dc][:], scalar=0.5, in1=xT[:, dc, :], op0=ALU.mult, op1=ALU.add),
                reads=[psr[dc], xr[dc]], writes=[xr[dc]])

    def ple(l, t, part):
        xT, xr, hT, hr, eT, er = TL["xT"], TL["xr"], TL["hT"], TL["hr"], TL["eT"], TL["er"]
        pTf, pTb, rs, rsr = TL["pTf"], TL["pTb"], TL["rs"], TL["rsr"]
        if part == "pre":
            ple_pre(l, t)
            return
        ple_main(l)

    def ple_pre(l, t):
        xT, xr, hT, hr, eT, er = TL["xT"], TL["xr"], TL["hT"], TL["hr"], TL["eT"], TL["er"]
        pTf, pTb, rs, rsr = TL["pTf"], TL["pTb"], TL["rs"], TL["rsr"]
        mk.dma("sp", out=pTf[:], in_=pT_in[l, :, t * TT:(t + 1) * TT].rearrange("(k p) s -> p k s", p=128), writes=[TL["pTfr"]])
        mk.op("act", lambda: nc.scalar.copy(out=pTb[:], in_=pTf[:]), reads=[TL["pTfr"]], writes=[TL["pTbr"]])
        for dc in range(8):
            w, wr_ = TL["plewb"][dc % 2]
            mk.dma("sp", out=w[:].rearrange("p k j -> p (k j)"), in_=plew_s[l, dc], reads=[wres], writes=[wr_])
            p, pr = next_ps()
            mm8(p[:], pr, w, wr_, lambda k: pTb[:, k, :], [TL["pTbr"]], nk=2)
            mk.op("act", lambda: nc.scalar.copy(out=eT[:, dc, :], in_=p[:]), reads=[pr], writes=[er[dc]])
        rstd_from(er, eT[:], 1.0 / D)
        for k in range(8):
            mk.op("dve", lambda: nc.vector.scalar_tensor_tensor(
                out=eT[:, k, :], in0=eT[:, k, :], scalar=gains_sb[:, 3, l, k:k + 1], in1=rs[:],
                op0=ALU.mult, op1=ALU.mult), reads=[er[k], rsr, cres], writes=[er[k]])
        rmsnorm_x(4, l)

    def ple_main(l):
        xT, xr, hT, hr, eT, er = TL["xT"], TL["xr"], TL["hT"], TL["hr"], TL["eT"], TL["er"]
        for dc in range(8):
            wg, wgr = wslot("wgb")
            mk.dma("sp", out=wg[:].rearrange("p k j -> p (k j)"), in_=pgw_s[l, dc], reads=[wres], writes=[wgr])
            p, pr = next_ps()
            mm8(p[:], pr, wg, wgr, lambda k: hT[:, k, :], [hr])
            s, sr = TL["sg"][dc % 2]
            mk.op("act", lambda: nc.scalar.activation(out=s[:], in_=p[:], func=AF.Sigmoid), reads=[pr], writes=[sr])
            mk.op("dve", lambda: nc.vector.tensor_tensor(out=s[:], in0=s[:], in1=eT[:, dc, :], op=ALU.mult),
                  reads=[sr, er[dc]], writes=[sr])
            mk.op("dve", lambda: nc.vector.tensor_tensor(out=xT[:, dc, :], in0=xT[:, dc, :], in1=s[:], op=ALU.add),
                  reads=[sr, xr[dc]], writes=[xr[dc]])
            if dc == 1:
                run_hook()

    def outproj(l, t):
        xT, xr, mT, mr = TL["xT"], TL["xr"], TL["hT"], TL["hr"]
        mk.dma("sp", out=mT[:], in_=m_d[:, t * TT:(t + 1) * TT].rearrange("(k p) s -> p k s", p=128), writes=[mr])
        for dc in range(8):
            wg, wgr = wslot("wgb")
            mk.dma("sp", out=wg[:].rearrange("p k j -> p (k j)"), in_=wout_s[l, dc], reads=[wres], writes=[wgr])
            p, pr = next_ps()
            mm8(p[:], pr, wg, wgr, lambda k: mT[:, k, :], [mr])
            mk.op("dve", lambda: nc.vector.tensor_tensor(out=xT[:, dc, :], in0=p[:], in1=xT[:, dc, :], op=ALU.add),
                  reads=[pr, xr[dc]], writes=[xr[dc]])

    def inproj_even(l, t, f_sb, f_r):
        j = l // 2
        hT, hr = TL["hT"], TL["hr"]
        tsl = slice(t * TT, (t + 1) * TT)
        for c in range(8):
            wg, wgr = wslot("wgb")
            mk.dma("sp", out=wg[:].rearrange("p k j -> p (k j)"), in_=evA_s[j, c], reads=[wres], writes=[wgr])
            p, pr = next_ps()
            mm8(p[:], pr, wg, wgr, lambda k: hT[:, k, :], [hr])
            z, zr = TL["zst"][c % 2]
            mk.op("act", lambda: nc.scalar.copy(out=z[:], in_=p[:]), reads=[pr], writes=[zr])
            mk.dma("act", out=zA_d[c * 128:(c + 1) * 128, tsl], in_=z[:], reads=[zr])
        for g0 in range(0, 8, 4):
            grp = list(range(g0, g0 + 4))
            pm = {}
            for c in grp:
                wg, wgr = wslot("wgb")
                mk.dma("sp", out=wg[:].rearrange("p k j -> p (k j)"), in_=evQK_s[j, c], reads=[wres], writes=[wgr])
                pm[c] = next_ps()
                mm8(pm[c][0][:], pm[c][1], wg, wgr, lambda k: hT[:, k, :], [hr])
            for c in grp:
                sqb, sqbr = TL["sqb4"][c % 4]
                mk.op("act", lambda: nc.scalar.activation(out=sqb[:], in_=pm[c][0][:], func=AF.Square), reads=[pm[c][1]], writes=[sqbr])
            p2s = {}
            for c in grp:
                sqb, sqbr = TL["sqb4"][c % 4]
                p2s[c] = next_ps()
                mk.op("pe", lambda: nc.tensor.matmul(p2s[c][0][:], lhsT=bd_ones[:], rhs=sqb[:], start=True, stop=True),
                      reads=[sqbr, cres], writes=[p2s[c][1]])
            for c in grp:
                rs2, rs2r = TL["rs24"][c % 4]
                mk.op("act", lambda: nc.scalar.activation(out=rs2[:], in_=p2s[c][0][:], func=AF.Ln, scale=1.0 / 64, bias=eps_col[:]),
                      reads=[p2s[c][1], cres], writes=[rs2r])
            for c in grp:
                rs2, rs2r = TL["rs24"][c % 4]
                mk.op("act", lambda: nc.scalar.activation(out=rs2[:], in_=rs2[:], func=AF.Exp, scale=-0.5), reads=[rs2r], writes=[rs2r])
            for c in grp:
                rs2, rs2r = TL["rs24"][c % 4]
                q, qr = TL["qst"][c % 4]
                gcol = foxp[:, j, (0 if c < 4 else 1):(1 if c < 4 else 2)]
                mk.op("dve", lambda: nc.vector.scalar_tensor_tensor(out=q[:], in0=pm[c][0][:], scalar=gcol, in1=rs2[:],
                                                                    op0=ALU.mult, op1=ALU.mult),
                      reads=[pm[c][1], rs2r, cres], writes=[qr])
                mk.dma("act", out=qk_d[c * 128:(c + 1) * 128, tsl], in_=q[:], reads=[qr])
        wv, wvr, wf, wfr = TL["wv"], TL["wvr"], TL["wf"], TL["wfr"]
        if t == 0:
            mk.dma("sp", out=wv[:].rearrange("p k j -> p (k j)"), in_=evV_s[j, 0], reads=[wres], writes=[wvr])
            mk.dma("sp", out=wf[:].rearrange("p k j -> p (k j)"), in_=evF_s[j, 0], reads=[wres], writes=[wfr])
        for tb in range(TT // 128):
            p, pr = next_ps()
            for k in range(8):
                mk.op("pe", lambda: nc.tensor.matmul(p[:], lhsT=hT[:, k, tb * 128:(tb + 1) * 128], rhs=wv[:, k, :],
                                                     start=(k == 0), stop=(k == 7)),
                      reads=[wvr, hr], writes=[pr], inc=(k == 7))
            q, qr = TL["qst"][tb % 2]
            mk.op("act", lambda: nc.scalar.copy(out=q[:], in_=p[:]), reads=[pr], writes=[qr])
            mk.dma("act", out=v_d[t * TT + tb * 128:t * TT + (tb + 1) * 128, :], in_=q[:], reads=[qr])
        p, pr = next_ps()
        for k in range(8):
            mk.op("pe", lambda: nc.tensor.matmul(p[0:8, :], lhsT=wf[:, k, :], rhs=hT[:, k, :], start=(k == 0), stop=(k == 7)),
                  reads=[wfr, hr], writes=[pr], inc=(k == 7))
        mk.op("act", lambda: nc.scalar.copy(out=f_sb[:, tsl], in_=p[0:8, :]), reads=[pr], writes=[f_r])


    def inproj_odd(l, t):
        j = l // 2
        hT, hr = TL["hT"], TL["hr"]
        tsl = slice(t * TT, (t + 1) * TT)
        ropeT, ropeTr = TL["ropeT"], TL["ropeTr"]
        mk.dma("sp", out=ropeT[:, 0, :], in_=ropec_in[:, tsl], writes=[ropeTr])
        mk.dma("sp", out=ropeT[:, 1, :], in_=ropes_in[:, tsl], writes=[ropeTr])
        for grp in [[0, 1], [2, 3], [4]]:
            pm, pw_ = {}, {}
            for c in grp:
                wi, wsi = (c, 4 + c) if c < 4 else (8, 9)
                wg, wgr = wslot("wgb")
                mk.dma("sp", out=wg[:].rearrange("p k j -> p (k j)"), in_=od_s[j, wi], reads=[wres], writes=[wgr])
                wu, wur = wslot("wub")
                mk.dma("sp", out=wu[:].rearrange("p k j -> p (k j)"), in_=od_s[j, wsi], reads=[wres], writes=[wur])
                pm[c] = next_ps()
                mm8(pm[c][0][:], pm[c][1], wg, wgr, lambda k: hT[:, k, :], [hr])
                pw_[c] = next_ps()
                mm8(pw_[c][0][:], pw_[c][1], wu, wur, lambda k: hT[:, k, :], [hr])
            for c in grp:
                sqb, sqbr = TL["sqb4"][c % 4]
                mk.op("act", lambda: nc.scalar.activation(out=sqb[:], in_=pm[c][0][:], func=AF.Square), reads=[pm[c][1]], writes=[sqbr])
            p2s = {}
            for c in grp:
                sqb, sqbr = TL["sqb4"][c % 4]
                p2s[c] = next_ps()
                mk.op("pe", lambda: nc.tensor.matmul(p2s[c][0][:], lhsT=bd_ones[:], rhs=sqb[:], start=True, stop=True),
                      reads=[sqbr, cres], writes=[p2s[c][1]])
            for c in grp:
                rs2, rs2r = TL["rs24"][c % 4]
                mk.op("act", lambda: nc.scalar.activation(out=rs2[:], in_=p2s[c][0][:], func=AF.Ln, scale=1.0 / 64, bias=eps_col[:]),
                      reads=[p2s[c][1], cres], writes=[rs2r])
            for c in grp:
                rs2, rs2r = TL["rs24"][c % 4]
                mk.op("act", lambda: nc.scalar.activation(out=rs2[:], in_=rs2[:], func=AF.Exp, scale=-0.5), reads=[rs2r], writes=[rs2r])
            for c in grp:
                g0 = 0 if c < 4 else 2
                rs2, rs2r = TL["rs24"][c % 4]
                qa, qar = (TL["qa4"] + TL["zst"])[(2 * c) % 4]
                qb, qbr = (TL["qa4"] + TL["zst"])[(2 * c + 1) % 4]
                mk.op("dve", lambda: nc.vector.scalar_tensor_tensor(out=qa[:], in0=pm[c][0][:], scalar=swap[:, j, g0:g0 + 1], in1=rs2[:],
                                                                    op0=ALU.mult, op1=ALU.mult),
                      reads=[pm[c][1], rs2r, cres], writes=[qar])
                mk.op("dve", lambda: nc.vector.scalar_tensor_tensor(out=qb[:], in0=pw_[c][0][:], scalar=swap[:, j, g0 + 1:g0 + 2], in1=rs2[:],
                                                                    op0=ALU.mult, op1=ALU.mult),
                      reads=[pw_[c][1], rs2r, cres], writes=[qbr])
                mk.op("dve", lambda: nc.vector.tensor_tensor(out=qa[:], in0=qa[:], in1=ropeT[:, 0, :], op=ALU.mult),
                      reads=[qar, ropeTr], writes=[qar])
                mk.op("pool", lambda: nc.gpsimd.tensor_tensor(out=qb[:], in0=qb[:], in1=ropeT[:, 1, :], op=ALU.mult),
                      reads=[qbr, ropeTr], writes=[qbr])
                q, qr = TL["qst"][c % 4]
                mk.op("dve", lambda: nc.vector.tensor_tensor(out=q[:], in0=qa[:], in1=qb[:], op=ALU.add),
                      reads=[qar, qbr], writes=[qr])
                r0 = c * 128 if c < 4 else 512
                mk.dma("act", out=qk_d[r0:r0 + 128, tsl], in_=q[:], reads=[qr])
        wv2, wv2r = TL["wv2"], TL["wv2r"]
        if t == 0:
            mk.dma("sp", out=wv2[:].rearrange("p k j -> p (k j)"), in_=od_s[j, 10], reads=[wres], writes=[wv2r])
        for tb in range(TT // 128):
            p, pr = next_ps()
            for k in range(8):
                mk.op("pe", lambda: nc.tensor.matmul(p[:, 0:128], lhsT=hT[:, k, tb * 128:(tb + 1) * 128], rhs=wv2[:, k, :],
                                                     start=(k == 0), stop=(k == 7)),
                      reads=[wv2r, hr], writes=[pr], inc=(k == 7))
            q, qr = TL["qst"][tb % 2]
            mk.op("act", lambda: nc.scalar.copy(out=q[:, 0:128], in_=p[:, 0:128]), reads=[pr], writes=[qr])
            mk.dma("act", out=v_d[t * TT + tb * 128:t * TT + (tb + 1) * 128, 0:128], in_=q[:, 0:128], reads=[qr])
        for c in range(4):
            wg, wgr = wslot("wgb")
            mk.dma("sp", out=wg[:].rearrange("p k j -> p (k j)"), in_=od_s[j, 11 + c], reads=[wres], writes=[wgr])
            p, pr = next_ps()
            mm8(p[:], pr, wg, wgr, lambda k: hT[:, k, :], [hr])
            z, zr = TL["zst"][c % 2]
            mk.op("act", lambda: nc.scalar.copy(out=z[:], in_=p[:]), reads=[pr], writes=[zr])
            mk.dma("act", out=zA_d[c * 128:(c + 1) * 128, tsl], in_=z[:], reads=[zr])

    def swa_core(l):
        j = l // 2
        a = Arena()
        Kt = [(a.sb("sKt%d" % i, [64, S], BF16), Res()) for i in range(2)]
        Va = [(a.sb("sVa%d" % i, [128, NB, 128], BF16), Res()) for i in range(2)]
        Qh = [(a.sb("sQh%d" % i, [64, S], BF16), Res()) for i in range(2)]
        pt = [(a.sb("spt%d" % i, [128, 256], BF16), Res()) for i in range(4)]
        rd = [(a.sb("srd%d" % i, [128, 512], F32), Res()) for i in range(2)]
        obs = [(a.sb("sobs%d" % i, [64, 512], BF16), Res()) for i in range(2)]
        bmf = a.sb("bmf", [128, 256], F32)
        bm = a.sb("bm", [128, 256], BF16)
        esink = a.sb("esink", [128, 8], F32)
        R = {n: Res(n) for n in ["bmf", "bm", "esink"]}
        mk.dma("sp", out=bmf[:], in_=bandmask_in, writes=[R["bmf"]])
        mk.op("dve", lambda: nc.vector.tensor_copy(out=bm[:], in_=bmf[:]), reads=[R["bmf"]], writes=[R["bm"]])
        mk.op("act", lambda: nc.scalar.activation(out=esink[:], in_=swap[:, j, 4:12], func=AF.Exp), reads=[cres], writes=[R["esink"]])
        for i in range(2):
            mk.op("pool", lambda: nc.gpsimd.memset(Va[i][0][:, :, 64:128], 1.0), writes=[Va[i][1]])
        ptc = 0
        oq = 0
        for g in range(2):
            Kg, Kr = Kt[g % 2]
            V, Vr = Va[g % 2]
            mk.dma("sp", out=Kg[:], in_=qk_d[512 + g * 64:512 + (g + 1) * 64, :], writes=[Kr])
            mk.dma("sp", out=V[:, :, 0:64], in_=v_d[:, g * 64:(g + 1) * 64].rearrange("(j p) d -> p j d", p=128), writes=[Vr])
            for hh in range(4):
                h = g * 4 + hh
                Q, Qr = Qh[h % 2]
                mk.dma("sp", out=Q[:], in_=qk_d[h * 64:(h + 1) * 64, :], writes=[Qr])
                pump(2, ("act", "dve"))
                Pprev = None

                def sqk(J):
                    N = 256 if J < NB - 1 else 128
                    pS, pSr = next_ps(6)
                    mk.op("pe", lambda: nc.tensor.matmul(pS[:, 0:N], lhsT=Kg[:, J * 128:(J + 1) * 128],
                                                         rhs=Q[:, J * 128:J * 128 + N], start=True, stop=True),
                          reads=[Kr, Qr], writes=[pSr])
                    return pS, pSr, N
                SLA = 3
                sq_ = [sqk(J0) for J0 in range(min(SLA, NB))]
                for J in range(NB):
                    pS, pSr, N = sq_.pop(0)
                    if J + SLA < NB:
                        sq_.append(sqk(J + SLA))
                    P, Pr = pt[ptc % 4]
                    ptc += 1
                    mk.op("act", lambda: nc.scalar.activation(out=P[:, 0:N], in_=pS[:, 0:N], func=AF.Exp, scale=0.125),
                          reads=[pSr], writes=[Pr])
                    mk.op("dve", lambda: nc.vector.tensor_tensor(out=P[:, 0:N], in0=P[:, 0:N], in1=bm[:, 0:N], op=ALU.mult),
                          reads=[Pr, R["bm"]], writes=[Pr])
                    po, por = ps[6 + (oq % 2)], psr[6 + (oq % 2)]
                    reg = slice((J % 4) * 128, (J % 4 + 1) * 128)
                    if J > 0:
                        Pp, Ppr = Pprev
                        mk.op("pe", lambda: nc.tensor.matmul(po[:, reg], lhsT=V[:, J - 1, :], rhs=Pp[:, 128:256],
                                                             start=True, stop=False),
                              reads=[Vr, Ppr], writes=[por], inc=False)
                    mk.op("pe", lambda: nc.tensor.matmul(po[:, reg], lhsT=V[:, J, :], rhs=P[:, 0:128],
                                                         start=(J == 0), stop=True),
                          reads=[Vr, Pr], writes=[por], inc=True)
                    Pprev = (P, Pr)
                    if J % 4 == 3:
                        r_, rr_ = rd[oq % 2]
                        o_, or_ = obs[oq % 2]
                        mk.op("dve", lambda: nc.vector.tensor_scalar(out=r_[64:128, :], in0=po[64:128, :],
                                                                     scalar1=esink[64:128, h:h + 1], scalar2=None, op0=ALU.add),
                              reads=[por, R["esink"]], writes=[rr_])
                        mk.op("dve", lambda: nc.vector.reciprocal(out=r_[64:128, :], in_=r_[64:128, :]), reads=[rr_], writes=[rr_])
                        mk.op("dve", lambda: nc.vector.tensor_tensor(out=o_[:], in0=po[0:64, :], in1=r_[64:128, :], op=ALU.mult),
                              reads=[por, rr_], writes=[or_])
                        mk.dma("sp", out=m_d[h * 64:(h + 1) * 64, (J - 3) * 128:(J + 1) * 128], in_=o_[:], reads=[or_])
                        oq += 1
        a.close()

    def s5_core(l):
        j = l // 2
        PL = min(S, 1024)
        NPC = S // PL
        I32 = mybir.dt.int32
        TWO_PI = 2.0 * math.pi
        MAGIC = 12582912.0
        a = Arena()
        magic = a.sb("magic", [128, 1], F32)
        lam = a.sb("lam", [128, 16, 3], F32)
        P_ = {n: a.sb("p_" + n, [128, 16], F32) for n in
              ["dt", "lr", "r", "f", "F1", "cs", "sn", "t0", "t1", "t2", "sre", "sim", "nsim"]}
        pi32 = a.sb("pi32", [128, 16], I32)
        pr_ = Res("params")
        lhsB = a.sb("lhsB", [128, 16, 2, 128], BF16)
        lhsC = a.sb("lhsC", [128, 16, 3, 128], BF16)
        lr_ = Res("lhs")
        mk.dma("sp", out=lam[:], in_=s5lam_in[:, j], writes=[pr_])
        mk.op("dve", lambda: nc.vector.memset(magic[:], MAGIC), reads=[pr_], writes=[pr_])

        def po(e, fn):
            mk.op(e, fn, reads=[pr_], writes=[pr_])
        V = nc.vector
        po("act", lambda: nc.scalar.activation(out=P_["dt"][:], in_=lam[:, :, 2], func=AF.Exp))
        po("dve", lambda: V.tensor_tensor(out=P_["lr"][:], in0=lam[:, :, 0], in1=P_["dt"][:], op=ALU.mult))
        po("dve", lambda: V.tensor_tensor(out=P_["f"][:], in0=lam[:, :, 1], in1=P_["dt"][:], op=ALU.mult))
        po("dve", lambda: V.tensor_scalar(out=P_["f"][:], in0=P_["f"][:], scalar1=1.0 / TWO_PI, scalar2=None, op0=ALU.mult))
        po("act", lambda: nc.scalar.activation(out=P_["r"][:], in_=P_["lr"][:], func=AF.Exp))

        def frac(dst, src):
            po("pool", lambda: nc.gpsimd.tensor_copy(out=pi32[:], in_=src))
            po("pool", lambda: nc.gpsimd.tensor_copy(out=P_["t0"][:], in_=pi32[:]))
            po("dve", lambda: V.tensor_tensor(out=dst, in0=src, in1=P_["t0"][:], op=ALU.subtract))
        po("dve", lambda: V.tensor_scalar(out=P_["t1"][:], in0=P_["f"][:], scalar1=64.0, scalar2=None, op0=ALU.mult))
        frac(P_["F1"][:], P_["t1"][:])
        frac(P_["t2"][:], P_["f"][:])
        po("act", lambda: nc.scalar.activation(out=P_["sn"][:], in_=P_["t2"][:], func=AF.Sin, scale=TWO_PI))
        po("act", lambda: nc.scalar.activation(out=P_["t1"][:], in_=P_["t2"][:], func=AF.Abs))
        po("act", lambda: nc.scalar.activation(out=P_["cs"][:], in_=P_["t1"][:], func=AF.Sin, scale=-TWO_PI, bias=math.pi / 2))
        po("dve", lambda: V.tensor_tensor(out=P_["t1"][:], in0=P_["r"][:], in1=P_["cs"][:], op=ALU.mult))
        po("dve", lambda: V.tensor_scalar(out=P_["t1"][:], in0=P_["t1"][:], scalar1=-1.0, scalar2=None, op0=ALU.add))
        po("dve", lambda: V.tensor_tensor(out=P_["t2"][:], in0=P_["r"][:], in1=P_["sn"][:], op=ALU.mult))
        A_, B_ = lam[:, :, 0], lam[:, :, 1]
        po("dve", lambda: V.tensor_tensor(out=P_["sre"][:], in0=P_["t1"][:], in1=A_, op=ALU.mult))
        po("dve", lambda: V.tensor_tensor(out=P_["t0"][:], in0=P_["t2"][:], in1=B_, op=ALU.mult))
        po("dve", lambda: V.tensor_tensor(out=P_["sre"][:], in0=P_["sre"][:], in1=P_["t0"][:], op=ALU.add))
        po("dve", lambda: V.tensor_tensor(out=P_["sim"][:], in0=P_["t2"][:], in1=A_, op=ALU.mult))
        po("dve", lambda: V.tensor_tensor(out=P_["t0"][:], in0=P_["t1"][:], in1=B_, op=ALU.mult))
        po("dve", lambda: V.tensor_tensor(out=P_["sim"][:], in0=P_["sim"][:], in1=P_["t0"][:], op=ALU.subtract))
        po("dve", lambda: V.tensor_tensor(out=P_["t0"][:], in0=A_, in1=A_, op=ALU.mult))
        po("dve", lambda: V.tensor_tensor(out=P_["t1"][:], in0=B_, in1=B_, op=ALU.mult))
        po("dve", lambda: V.tensor_tensor(out=P_["t0"][:], in0=P_["t0"][:], in1=P_["t1"][:], op=ALU.add))
        po("dve", lambda: V.reciprocal(out=P_["t0"][:], in_=P_["t0"][:]))
        po("dve", lambda: V.tensor_tensor(out=P_["sre"][:], in0=P_["sre"][:], in1=P_["t0"][:], op=ALU.mult))
        po("dve", lambda: V.tensor_tensor(out=P_["sim"][:], in0=P_["sim"][:], in1=P_["t0"][:], op=ALU.mult))
        po("dve", lambda: V.tensor_scalar(out=P_["nsim"][:], in0=P_["sim"][:], scalar1=-1.0, scalar2=None, op0=ALU.mult))
        a2 = Arena()
        Bm = a2.sb("Bm", [128, 2, 16, 128], F32)
        Cm = a2.sb("Cm", [128, 2, 16, 128], F32)
        tb1 = a2.sb("tb1", [128, 128], F32)
        tb2 = a2.sb("tb2", [128, 128], F32)
        bmr, cmr, t1r, t2r = Res(), Res(), Res(), Res()
        for ri in range(2):
            mk.dma("sp", out=Bm[:, ri], in_=s5B_in[j, ri], writes=[bmr])
            mk.dma("sp", out=Cm[:, ri], in_=s5C_in[j, ri], writes=[cmr])
        for mc in range(16):
            col = lambda n: P_[n][:, mc:mc + 1]
            mk.op("dve", lambda: V.tensor_scalar(out=tb1[:], in0=Bm[:, 0, mc, :], scalar1=col("sre"), scalar2=None, op0=ALU.mult),
                  reads=[bmr, pr_], writes=[t1r])
            mk.op("dve", lambda: V.scalar_tensor_tensor(out=tb1[:], in0=Bm[:, 1, mc, :], scalar=col("nsim"), in1=tb1[:],
                                                        op0=ALU.mult, op1=ALU.add), reads=[bmr, pr_, t1r], writes=[t1r])
            mk.op("dve", lambda: V.tensor_scalar(out=tb2[:], in0=Bm[:, 1, mc, :], scalar1=col("sre"), scalar2=None, op0=ALU.mult),
                  reads=[bmr, pr_], writes=[t2r])
            mk.op("dve", lambda: V.scalar_tensor_tensor(out=tb2[:], in0=Bm[:, 0, mc, :], scalar=col("sim"), in1=tb2[:],
                                                        op0=ALU.mult, op1=ALU.add), reads=[bmr, pr_, t2r], writes=[t2r])
            for ri, (tt_, tr_) in enumerate([(tb1, t1r), (tb2, t2r)]):
                p, ppr = next_ps()
                mk.op("pe", lambda: nc.tensor.transpose(p[:, 0:128], tt_[:], ident[:]), reads=[tr_, cres], writes=[ppr])
                mk.op("act", lambda: nc.scalar.copy(out=lhsB[:, mc, ri, :], in_=p[:, 0:128]), reads=[ppr], writes=[lr_])
            for ri in range(2):
                p, ppr = next_ps()
                mk.op("pe", lambda: nc.tensor.transpose(p[:, 0:128], Cm[:, ri, mc, :], ident[:]), reads=[cmr, cres], writes=[ppr])
                mk.op("act", lambda: nc.scalar.activation(out=lhsC[:, mc, ri, :], in_=p[:, 0:128], func=AF.Copy,
                                                          scale=(1.0 if ri == 0 else -1.0)), reads=[ppr], writes=[lr_])
                if ri == 0:
                    mk.op("act", lambda: nc.scalar.activation(out=lhsC[:, mc, 2, :], in_=p[:, 0:128], func=AF.Copy, scale=-1.0),
                          reads=[ppr], writes=[lr_])
        a2.close()
        def mkset(tag):
            B = {}
            for nm in ["cosT", "sinT", "vv", "bre", "bim", "gre", "gim"]:
                B[nm] = a.sb(nm + tag, [128, PL], F32)
            B["hre"] = a.sb("hre" + tag, [128, PL], BF16)
            B["him"] = a.sb("him" + tag, [128, PL], BF16)
            B["h3"] = a.sb("h3" + tag, [128, PL], BF16)
            B["h4"] = a.sb("h4" + tag, [128, PL], BF16)
            B["tq"] = [(a.sb("tq%d%s" % (i, tag), [128, TT], F32), Res()) for i in range(2)]
            B["R"] = {n: Res(n) for n in ["cos", "sin", "vv", "bre", "bim", "gre", "gim", "hre", "him", "h3", "h4", "carry"]}
            return B
        SETS = [mkset("A"), mkset("B")]
        yacc = a.sb("yacc", [128, PL], F32)
        uf = a.sb("uf", [128, PL], F32)
        ub = a.sb("ub", [128, PL], BF16)
        carry = a.sb("carry", [128, 16, 2], F32)
        zb = a.sb("zb", [128, PL], F32)
        RS = {n: Res(n) for n in ["iota", "yacc", "uf", "ub", "zb"]}
        NTP = PL // TT

        lhsV = a.sb("lhsV", [2, 16, 128], F32)
        wm = a.sb("wm", [128, 16, 2], F32)
        wmr, lvr = Res("wm"), Res("lhsV")
        mk.op("dve", lambda: V.tensor_copy(out=wm[:, :, 0], in_=P_["F1"][:]), reads=[pr_], writes=[wmr])
        mk.op("dve", lambda: V.tensor_copy(out=wm[:, :, 1], in_=P_["f"][:]), reads=[pr_, wmr], writes=[wmr])
        for mc_ in range(16):
            pt_, ptr_ = next_ps()
            mk.op("pe", lambda: nc.tensor.transpose(pt_[0:2, 0:128], wm[:, mc_, :], ident[:]), reads=[wmr, cres], writes=[ptr_])
            mk.op("act", lambda: nc.scalar.copy(out=lhsV[:, mc_, :], in_=pt_[0:2, 0:128]), reads=[ptr_], writes=[lvr])
        iota2 = [a.sb("iota2_%d" % i, [2, PL], F32) for i in range(NPC)]
        for pc_ in range(NPC):
            mk.dma("sp", out=iota2[pc_][0:1, :], in_=iota_in[0:1, 0, pc_ * PL:(pc_ + 1) * PL], writes=[RS["iota"]])
            mk.dma("sp", out=iota2[pc_][1:2, :], in_=iota_in[0:1, 1, 0:PL], writes=[RS["iota"]])

        def s5_iter(pc, cc, m4, B):
            mc = cc * 4 + m4
            col = lambda n: P_[n][:, mc:mc + 1]
            cosT, sinT, vv, bre, bim, gre, gim, hre, him = (B[n] for n in ["cosT", "sinT", "vv", "bre", "bim", "gre", "gim", "hre", "him"])
            R = B["R"]
            tq = B["tq"]
            for t in range(NTP):
                sl = slice(t * TT, (t + 1) * TT)
                pv, pvr = next_ps()
                mk.op("pe", lambda: nc.tensor.matmul(pv[:], lhsT=lhsV[:, mc, :], rhs=iota2[pc][:, sl], start=True, stop=True),
                      reads=[lvr, RS["iota"]], writes=[pvr])
                mk.op("act", lambda: nc.scalar.activation(out=sinT[:, sl], in_=pv[:], func=AF.Identity, bias=magic[:], scale=1.0),
                      reads=[pvr, R["sin"], pr_], writes=[R["sin"]])
                yield
                mk.op("dve", lambda: V.scalar_tensor_tensor(out=vv[:, sl], in0=sinT[:, sl], scalar=MAGIC, in1=pv[:],
                                                            op0=ALU.subtract, op1=ALU.subtract),
                      reads=[R["sin"], pvr, R["vv"]], writes=[R["vv"]])
                yield
            mk.op("act", lambda: nc.scalar.activation(out=sinT[:], in_=vv[:], func=AF.Sin, scale=-TWO_PI),
                  reads=[R["vv"]], writes=[R["sin"]])
            yield
            mk.op("act", lambda: nc.scalar.activation(out=vv[:], in_=vv[:], func=AF.Abs), reads=[R["vv"]], writes=[R["vv"]])
            yield
            mk.op("act", lambda: nc.scalar.activation(out=cosT[:], in_=vv[:], func=AF.Sin, scale=-TWO_PI, bias=math.pi / 2),
                  reads=[R["vv"]], writes=[R["cos"]])
            yield
            for t in range(NTP):
                sl = slice(t * TT, (t + 1) * TT)
                p1, p1r = next_ps()
                p2, p2r = next_ps()
                mk.op("pe", lambda: nc.tensor.matmul(p1[:], lhsT=lhsB[:, mc, 0, :], rhs=ub[:, sl], start=True, stop=True),
                      reads=[lr_, RS["ub"]], writes=[p1r])
                mk.op("pe", lambda: nc.tensor.matmul(p2[:], lhsT=lhsB[:, mc, 1, :], rhs=ub[:, sl], start=True, stop=True),
                      reads=[lr_, RS["ub"]], writes=[p2r])
                q1, q1r = tq[0]
                q2, q2r = tq[1]
                yield
                mk.op("dve", lambda: V.tensor_tensor(out=bre[:, sl], in0=p1[:], in1=cosT[:, sl], op=ALU.mult),
                      reads=[p1r, R["cos"]], writes=[R["bre"]])
                yield
                mk.op("dve", lambda: V.tensor_tensor(out=q1[:], in0=p2[:], in1=sinT[:, sl], op=ALU.mult),
                      reads=[p2r, R["sin"]], writes=[q1r])
                yield
                mk.op("pool", lambda: nc.gpsimd.tensor_tensor(out=bre[:, sl], in0=bre[:, sl], in1=q1[:], op=ALU.add),
                      reads=[q1r, R["bre"]], writes=[R["bre"]])
                yield
                mk.op("dve", lambda: V.tensor_tensor(out=bim[:, sl], in0=p2[:], in1=cosT[:, sl], op=ALU.mult),
                      reads=[p2r, R["cos"]], writes=[R["bim"]])
                yield
                mk.op("dve", lambda: V.tensor_tensor(out=q2[:], in0=p1[:], in1=sinT[:, sl], op=ALU.mult),
                      reads=[p1r, R["sin"]], writes=[q2r])
                yield
                mk.op("pool", lambda: nc.gpsimd.tensor_tensor(out=bim[:, sl], in0=bim[:, sl], in1=q2[:], op=ALU.subtract),
                      reads=[q2r, R["bim"]], writes=[R["bim"]])
                yield
            ini_re = 0.0 if pc == 0 else carry[:, mc, 0:1]
            ini_im = 0.0 if pc == 0 else carry[:, mc, 1:2]
            rb = P_["r"][:, mc:mc + 1].to_broadcast([128, PL])
            mk.op("dve", lambda: V.tensor_tensor_scan(out=gre[:], data0=rb, data1=bre[:], initial=ini_re,
                                                      op0=ALU.mult, op1=ALU.add),
                  reads=[R["bre"], pr_, R["carry"]], writes=[R["gre"]])
            yield
            mk.op("dve", lambda: V.tensor_tensor_scan(out=gim[:], data0=rb, data1=bim[:], initial=ini_im,
                                                      op0=ALU.mult, op1=ALU.add),
                  reads=[R["bim"], pr_, R["carry"]], writes=[R["gim"]])
            yield
            if pc < NPC - 1:
                mk.op("act", lambda: nc.scalar.copy(out=carry[:, mc, 0:1], in_=gre[:, PL - 1:PL]), reads=[R["gre"]], writes=[R["carry"]])
                mk.op("act", lambda: nc.scalar.copy(out=carry[:, mc, 1:2], in_=gim[:, PL - 1:PL]), reads=[R["gim"]], writes=[R["carry"]])
                yield
            h3, h4 = B["h3"], B["h4"]
            mk.op("dve", lambda: V.tensor_tensor(out=hre[:], in0=gre[:], in1=cosT[:], op=ALU.mult),
                  reads=[R["gre"], R["cos"]], writes=[R["hre"]])
            yield
            mk.op("pool", lambda: nc.gpsimd.tensor_tensor(out=him[:], in0=gim[:], in1=sinT[:], op=ALU.mult),
                  reads=[R["gim"], R["sin"]], writes=[R["him"]])
            yield
            mk.op("pool", lambda: nc.gpsimd.tensor_tensor(out=h3[:], in0=gre[:], in1=sinT[:], op=ALU.mult),
                  reads=[R["gre"], R["sin"]], writes=[R["h3"]])
            yield
            mk.op("pool", lambda: nc.gpsimd.tensor_tensor(out=h4[:], in0=gim[:], in1=cosT[:], op=ALU.mult),
                  reads=[R["gim"], R["cos"]], writes=[R["h4"]])
            yield
            for t in range(NTP):
                sl = slice(t * TT, (t + 1) * TT)
                p1, p1r = next_ps()
                mk.op("pe", lambda: nc.tensor.matmul(p1[:], lhsT=lhsC[:, mc, 0, :], rhs=hre[:, sl], start=True, stop=False),
                      reads=[lr_, R["hre"]], writes=[p1r], inc=False)
                mk.op("pe", lambda: nc.tensor.matmul(p1[:], lhsT=lhsC[:, mc, 2, :], rhs=him[:, sl], start=False, stop=False),
                      reads=[lr_, R["him"]], writes=[p1r], inc=False)
                mk.op("pe", lambda: nc.tensor.matmul(p1[:], lhsT=lhsC[:, mc, 1, :], rhs=h3[:, sl], start=False, stop=False),
                      reads=[lr_, R["h3"]], writes=[p1r], inc=False)
                mk.op("pe", lambda: nc.tensor.matmul(p1[:], lhsT=lhsC[:, mc, 1, :], rhs=h4[:, sl], start=False, stop=True),
                      reads=[lr_, R["h4"]], writes=[p1r])
                if m4 == 0:
                    mk.op("act", lambda: nc.scalar.copy(out=yacc[:, sl], in_=p1[:]), reads=[p1r], writes=[RS["yacc"]])
                else:
                    mk.op("dve", lambda: V.tensor_tensor(out=yacc[:, sl], in0=p1[:], in1=yacc[:, sl], op=ALU.add),
                          reads=[p1r, RS["yacc"]], writes=[RS["yacc"]])
                yield

        def chain2(g1, g2):
            for _ in g1:
                yield
            for _ in g2:
                yield

        for pc in range(NPC):
            t0 = pc * PL
            for cc in range(4):
                mk.dma("sp", out=uf[:], in_=zA_d[cc * 128:(cc + 1) * 128, t0:t0 + PL], writes=[RS["uf"]])
                mk.op("act", lambda: nc.scalar.copy(out=ub[:], in_=uf[:]), reads=[RS["uf"]], writes=[RS["ub"]])
                pump(4, ("act",))
                gA = chain2(s5_iter(pc, cc, 0, SETS[0]), s5_iter(pc, cc, 2, SETS[0]))
                gB = chain2(s5_iter(pc, cc, 1, SETS[1]), s5_iter(pc, cc, 3, SETS[1]))
                dA = dB = False
                while not (dA and dB):
                    if not dA:
                        dA = next(gA, "done") == "done"
                    if not dB:
                        dB = next(gB, "done") == "done"
                mk.op("dve", lambda: V.scalar_tensor_tensor(out=yacc[:], in0=uf[:], scalar=s5d[:, j, cc, 0:1], in1=yacc[:],
                                                            op0=ALU.mult, op1=ALU.add),
                      reads=[RS["uf"], RS["yacc"], cres], writes=[RS["yacc"]])
                mk.op("act", lambda: nc.scalar.activation(out=zb[:], in_=yacc[:], func=AF.Gelu_apprx_tanh),
                      reads=[RS["yacc"]], writes=[RS["zb"]])
                mk.dma("sp", out=zA_d[512 + cc * 128:512 + (cc + 1) * 128, t0:t0 + PL], in_=zb[:], reads=[RS["zb"]])
        a.close()
        a = Arena()
        zf = a.sb("zf", [128, 4, TT], F32)
        zbf = a.sb("zbf", [128, 4, TT], BF16)
        gw = a.sb("gw", [128, 4, 4, 128], BF16)
        sgt = [(a.sb("gsg%d" % i, [128, TT], F32), Res()) for i in range(2)]
        ot = [(a.sb("got%d" % i, [128, TT], BF16), Res()) for i in range(2)]
        zfr, zbr, gwr = Res(), Res(), Res()
        mk.dma("sp", out=gw[:].rearrange("p n k c -> p n (k c)"), in_=glu_s[j].rearrange("n p x -> p n x"), writes=[gwr])
        for t in range(NT):
            tsl = slice(t * TT, (t + 1) * TT)
            mk.dma("sp", out=zf[:], in_=zA_d[512:1024, tsl].rearrange("(k p) s -> p k s", p=128), writes=[zfr])
            mk.op("act", lambda: nc.scalar.copy(out=zbf[:], in_=zf[:]), reads=[zfr], writes=[zbr])
            for cc in range(4):
                p, ppr = next_ps()
                for k in range(4):
                    mk.op("pe", lambda: nc.tensor.matmul(p[:], lhsT=gw[:, cc, k, :], rhs=zbf[:, k, :], start=(k == 0), stop=(k == 3)),
                          reads=[gwr, zbr], writes=[ppr], inc=(k == 3))
                sgx, sgr = sgt[cc % 2]
                o_, or_ = ot[cc % 2]
                mk.op("act", lambda: nc.scalar.activation(out=sgx[:], in_=p[:], func=AF.Sigmoid, bias=s5d[:, j, cc, 1:2]),
                      reads=[ppr, cres], writes=[sgr])
                mk.op("dve", lambda: nc.vector.tensor_tensor(out=o_[:], in0=sgx[:], in1=zf[:, cc, :], op=ALU.mult),
                      reads=[sgr, zfr], writes=[or_])
                mk.dma("sp", out=m_d[512 + cc * 128:512 + (cc + 1) * 128, tsl], in_=o_[:], reads=[or_])
        a.close()

    def load_x(src, t):
        mk.dma("sp", out=TL["xT"][:], in_=src[:, t * TT:(t + 1) * TT].rearrange("(k p) s -> p k s", p=128),
               writes=TL["xr"])

    def store_x(dst, t):
        mk.dma("sp", out=dst[:, t * TT:(t + 1) * TT].rearrange("(k p) s -> p k s", p=128), in_=TL["xT"][:],
               reads=TL["xr"])

    def gelu_tanh(a, out, x, xres_, outres, tmp, tmpres, n):
        mk.op("act", lambda: nc.scalar.activation(out=tmp, in_=x, func=AF.Square), reads=[xres_], writes=[tmpres])
        mk.op("dve", lambda: nc.vector.tensor_scalar(out=tmp, in0=tmp, scalar1=0.044715, scalar2=1.0,
                                                     op0=ALU.mult, op1=ALU.add), reads=[tmpres], writes=[tmpres])
        mk.op("dve", lambda: nc.vector.tensor_tensor(out=tmp, in0=tmp, in1=x, op=ALU.mult), reads=[tmpres, xres_], writes=[tmpres])
        mk.op("act", lambda: nc.scalar.activation(out=tmp, in_=tmp, func=AF.Sigmoid, scale=1.5957691216057308),
              reads=[tmpres], writes=[tmpres])
        mk.op("dve", lambda: nc.vector.tensor_tensor(out=out, in0=tmp, in1=x, op=ALU.mult), reads=[tmpres, xres_], writes=[outres])

    def rglru_gen(l, nps, ccs):
        j = l // 2
        HS = S // 2 if S >= 2048 else S
        NH = S // HS
        NTH = HS // TT
        a = Arena()
        xa = a.sb("xa", [128, HS + 3], F32)
        ya = a.sb("ya", [128, HS], F32)
        xc = a.sb("xc", [128, HS], F32)
        xcb = a.sb("xcb", [128, HS], BF16)
        rr = a.sb("rr", [128, HS], F32)
        ii = a.sb("ii", [128, HS], F32)
        t1 = a.sb("t1", [128, HS], F32)
        ob = a.sb("ob", [128, HS], BF16)
        wst = a.sb("wst", [128, 2, 128], F32)
        wbd = a.sb("wbd", [128, 2, 128], BF16)
        sc = a.sb("sc", [128, 4], F32)
        hc = a.sb("hc", [128, 4], F32)
        R = {n: Res(n) for n in ["xa", "ya", "xc", "xcb", "rr", "ii", "t1", "ob", "wst", "wbd", "sc", "hc"]}
        mk.op("act", lambda: nc.scalar.activation(out=sc[:], in_=lrup[:, j, :, 7], func=AF.Exp), reads=[cres], writes=[R["sc"]])
        mk.op("act", lambda: nc.scalar.activation(out=sc[:], in_=sc[:], func=AF.Ln, bias=1.0), reads=[R["sc"]], writes=[R["sc"]])
        mk.op("dve", lambda: nc.vector.tensor_scalar(out=sc[:], in0=sc[:], scalar1=-8.0, scalar2=None, op0=ALU.mult),
              reads=[R["sc"]], writes=[R["sc"]])
        yield
        for cc in ccs:
            prm = lambda i: lrup[:, j, cc, i:i + 1]
            mk.dma("sp", out=wst[:], in_=lru_wbd[j, :, cc].rearrange("a p q -> p a q"), writes=[R["wst"]])
            mk.op("pool", lambda: nc.gpsimd.tensor_copy(out=wbd[:], in_=wst[:]), reads=[R["wst"]], writes=[R["wbd"]])
            yield
            for hf in range(NH):
                h0 = hf * HS
                if hf == 0:
                    mk.op("dve", lambda: nc.vector.memset(xa[:, 0:3], 0.0), writes=[R["xa"]])
                    mk.dma("sp", out=xa[:, 3:3 + HS], in_=zA_d[cc * 128:(cc + 1) * 128, 0:HS], writes=[R["xa"]])
                else:
                    mk.dma("sp", out=xa[:, 0:3 + HS], in_=zA_d[cc * 128:(cc + 1) * 128, h0 - 3:h0 + HS], writes=[R["xa"]])
                mk.dma("sp", out=ya[:], in_=zA_d[512 + cc * 128:512 + (cc + 1) * 128, h0:h0 + HS], writes=[R["ya"]])
                yield
                mk.op("dve", lambda: nc.vector.tensor_scalar(out=xc[:], in0=xa[:, 0:HS], scalar1=prm(0), scalar2=prm(4),
                                                             op0=ALU.mult, op1=ALU.add), reads=[R["xa"], cres], writes=[R["xc"]])
                yield
                for tap in range(1, 4):
                    mk.op("dve", lambda: nc.vector.scalar_tensor_tensor(out=xc[:], in0=xa[:, tap:tap + HS], scalar=prm(tap), in1=xc[:],
                                                                        op0=ALU.mult, op1=ALU.add),
                          reads=[R["xa"], R["xc"], cres], writes=[R["xc"]])
                    yield
                mk.op("act", lambda: nc.scalar.copy(out=xcb[:], in_=xc[:]), reads=[R["xc"]], writes=[R["xcb"]])
                yield
                for t in range(NTH):
                    tsl = slice(t * TT, (t + 1) * TT)
                    p, pr = next_ps(nps)
                    mk.op("pe", lambda: nc.tensor.matmul(p[:], lhsT=wbd[:, 0, :], rhs=xcb[:, tsl], start=True, stop=True),
                          reads=[R["wbd"], R["xcb"]], writes=[pr])
                    mk.op("act", lambda: nc.scalar.activation(out=rr[:, tsl], in_=p[:], func=AF.Sigmoid, bias=prm(5)),
                          reads=[pr, cres], writes=[R["rr"]])
                    yield
                    p, pr = next_ps(nps)
                    mk.op("pe", lambda: nc.tensor.matmul(p[:], lhsT=wbd[:, 1, :], rhs=xcb[:, tsl], start=True, stop=True),
                          reads=[R["wbd"], R["xcb"]], writes=[pr])
                    mk.op("act", lambda: nc.scalar.activation(out=ii[:, tsl], in_=p[:], func=AF.Sigmoid, bias=prm(6)),
                          reads=[pr, cres], writes=[R["ii"]])
                    yield
                mk.op("act", lambda: nc.scalar.activation(out=rr[:], in_=rr[:], func=AF.Exp, scale=sc[:, cc:cc + 1]),
                      reads=[R["rr"], R["sc"]], writes=[R["rr"]])
                yield
                mk.op("act", lambda: nc.scalar.activation(out=t1[:], in_=rr[:], func=AF.Square), reads=[R["rr"]], writes=[R["t1"]])
                yield
                mk.op("act", lambda: nc.scalar.activation(out=t1[:], in_=t1[:], func=AF.Sqrt, scale=-1.0, bias=1.0),
                      reads=[R["t1"]], writes=[R["t1"]])
                yield
                mk.op("dve", lambda: nc.vector.tensor_tensor(out=t1[:], in0=t1[:], in1=ii[:], op=ALU.mult),
                      reads=[R["t1"], R["ii"]], writes=[R["t1"]])
                yield
                mk.op("dve", lambda: nc.vector.tensor_tensor(out=t1[:], in0=t1[:], in1=xc[:], op=ALU.mult),
                      reads=[R["t1"], R["xc"]], writes=[R["t1"]])
                yield
                ini = 0.0 if hf == 0 else hc[:, cc:cc + 1]
                mk.op("dve", lambda: nc.vector.tensor_tensor_scan(out=ii[:], data0=rr[:], data1=t1[:], initial=ini,
                                                                  op0=ALU.mult, op1=ALU.add),
                      reads=[R["rr"], R["t1"], R["ii"], R["hc"]], writes=[R["ii"]])
                yield
                if hf < NH - 1:
                    mk.op("act", lambda: nc.scalar.copy(out=hc[:, cc:cc + 1], in_=ii[:, HS - 1:HS]), reads=[R["ii"]], writes=[R["hc"]])
                mk.op("act", lambda: nc.scalar.activation(out=t1[:], in_=ya[:], func=AF.Gelu_apprx_tanh),
                      reads=[R["ya"], R["t1"]], writes=[R["t1"]])
                yield
                mk.op("dve", lambda: nc.vector.tensor_tensor(out=ob[:], in0=t1[:], in1=ii[:], op=ALU.mult),
                      reads=[R["t1"], R["ii"]], writes=[R["ob"]])
                mk.dma("sp", out=m_d[cc * 128:(cc + 1) * 128, h0:h0 + HS], in_=ob[:], reads=[R["ob"]])
                yield
            pump(4, ("pool",))
            yield
        yield "closing"
        a.close()

    def rglru_core(l):
        ga = rglru_gen(l, 8, [0, 1])
        gb = rglru_gen(l, 8, [2, 3])
        da = db = False
        while not (da and db):
            if not da:
                da = next(ga, "done") in ("done", "closing")
            if not db:
                db = next(gb, "done") in ("done", "closing")
        for g in (gb, ga):
            for _ in g:
                pass

    def fox_core(l, f_sb, f_r):
        j = l // 2
        a = Arena()
        Csb = a.sb("Csb", [8, S], F32)
        onesf = a.sb("onesf", [8, S], F32)
        tmpf = a.sb("tmpf", [8, S], F32)
        cb = [a.sb("cb%d" % i, [8, S], BF16) for i in range(3)]
        nbf = a.sb("nbf", [8, 1], F32)
        Ccol = a.sb("Ccol", [128, NB, 8], F32)
        Qa = [(a.sb("Qa%d" % i, [67, S], BF16), Res()) for i in range(2)]
        Ka = [(a.sb("Ka%d" % i, [67, S], BF16), Res()) for i in range(2)]
        Va = [(a.sb("Va%d" % i, [128, NB, 128], BF16), Res()) for i in range(2)]
        pt = [(a.sb("pt%d" % i, [128, 512], BF16), Res()) for i in range(4)]
        rd = [(a.sb("rd%d" % i, [128, 512], F32), Res()) for i in range(2)]
        obs = [(a.sb("obs%d" % i, [64, 512], BF16), Res()) for i in range(2)]
        R = {n: Res(n) for n in ["C", "ones", "tmp", "cb", "nbf", "Ccol", "csd"]}
        for i in range(2):
            mk.op("dve", lambda: nc.vector.memset(Ka[i][0][64:67, :], 1.0), writes=[Ka[i][1]])
            mk.op("pool", lambda: nc.gpsimd.memset(Va[i][0][:, :, 64:128], 1.0), writes=[Va[i][1]])
        mk.op("pool", lambda: nc.gpsimd.memset(onesf[:], 1.0), writes=[R["ones"]])
        mk.op("dve", lambda: nc.vector.tensor_scalar(out=nbf[:], in0=foxp[0:8, j, 2:3], scalar1=-1.0, scalar2=None, op0=ALU.mult),
              reads=[cres], writes=[R["nbf"]])
        mk.op("act", lambda: nc.scalar.activation(out=f_sb[:], in_=f_sb[:], func=AF.Exp, scale=-1.0, bias=nbf[:]),
              reads=[f_r, R["nbf"]], writes=[f_r])
        mk.op("act", lambda: nc.scalar.activation(out=f_sb[:], in_=f_sb[:], func=AF.Ln, bias=1.0), reads=[f_r], writes=[f_r])
        mk.op("dve", lambda: nc.vector.tensor_tensor_scan(out=Csb[:], data0=onesf[:], data1=f_sb[:], initial=0.0,
                                                          op0=ALU.mult, op1=ALU.add),
              reads=[f_r, R["ones"]], writes=[R["C"]])
        pc, pcr = ps[0], psr[0]
        for J in range(NB):
            mk.op("pe", lambda: nc.tensor.transpose(pc[:, J * 8:(J + 1) * 8], Csb[:, J * 128:(J + 1) * 128], ident[0:8, 0:8]),
                  reads=[R["C"], cres], writes=[pcr], inc=(J == NB - 1))
        mk.op("dve", lambda: nc.vector.tensor_copy(out=Ccol[:].rearrange("p j h -> p (j h)"), in_=pc[:, 0:NB * 8]),
              reads=[pcr], writes=[R["Ccol"]])
        mk.op("dve", lambda: nc.vector.tensor_scalar(out=tmpf[:], in0=Csb[:], scalar1=-8.0, scalar2=None, op0=ALU.mult),
              reads=[R["C"]], writes=[R["tmp"]])
        for i in range(3):
            mk.op("dve", lambda: nc.vector.tensor_copy(out=cb[i][:], in_=tmpf[:]), reads=[R["tmp"]], writes=[R["cb"]])
            if i < 2:
                mk.op("dve", lambda: nc.vector.tensor_tensor(out=tmpf[:], in0=tmpf[:], in1=cb[i][:], op=ALU.subtract),
                      reads=[R["tmp"], R["cb"]], writes=[R["tmp"]])
            mk.dma("sp", out=cs_d[i], in_=cb[i][:], reads=[R["cb"]], writes=[R["csd"]])
        oq = 0
        ptc = [0]
        for h in range(8):
            Q, Qr = Qa[h % 2]
            Kt, Kr = Ka[h % 2]
            V, Vr = Va[h % 2]
            mk.dma("sp", out=Q[0:64, :], in_=qk_d[h * 64:(h + 1) * 64, :], writes=[Qr])
            mk.dma("sp", out=Q[64:67, :], in_=cs_d[:, h, :], reads=[R["csd"]], writes=[Qr])
            mk.dma("sp", out=Kt[0:64, :], in_=qk_d[512 + h * 64:512 + (h + 1) * 64, :], writes=[Kr])
            mk.dma("sp", out=V[:, :, 0:64], in_=v_d[:, h * 64:(h + 1) * 64].rearrange("(j p) d -> p j d", p=128), writes=[Vr])
            items = [(I, J) for I in range(NT) for J in range(4 * I + 4)]

            def qk(I, J):
                b = J - 4 * I
                c0 = b * 128 if b > 0 else 0
                pS, pSr = next_ps(6)
                mk.op("pe", lambda: nc.tensor.matmul(pS[:, c0:512], lhsT=Kt[0:67, J * 128:(J + 1) * 128],
                                                     rhs=Q[0:67, I * 512 + c0:(I + 1) * 512], start=True, stop=True),
                      reads=[Kr, Qr], writes=[pSr])
                return pS, pSr, b, c0
            LA = 3
            qq = [qk(*items[i0]) for i0 in range(min(LA, len(items)))]
            for ii_, (I, J) in enumerate(items):
                nJ = 4 * I + 4
                po, por = ps[6 + (oq % 2)], psr[6 + (oq % 2)]
                pS, pSr, b, c0 = qq.pop(0)
                if ii_ + LA < len(items):
                    qq.append(qk(*items[ii_ + LA]))
                if b >= 0:
                    mk.op("dve", lambda: nc.vector.tensor_tensor(out=pS[:, c0:c0 + 128], in0=pS[:, c0:c0 + 128],
                                                                 in1=negmask[:], op=ALU.add),
                          reads=[pSr, cres], writes=[pSr])
                P, Pr = pt[ptc[0] % 4]
                ptc[0] += 1
                mk.op("act", lambda: nc.scalar.activation(out=P[:, c0:512], in_=pS[:, c0:512], func=AF.Exp,
                                                          scale=0.125, bias=Ccol[:, J, h:h + 1]),
                      reads=[pSr, R["Ccol"]], writes=[Pr])
                mk.op("pe", lambda: nc.tensor.matmul(po[:, c0:512], lhsT=V[:, J, :], rhs=P[:, c0:512],
                                                     start=(J == 0), stop=(J == nJ - 1)),
                      reads=[Vr, Pr], writes=[por], inc=True)
                if J < nJ - 1:
                    continue
                r_, rr_ = rd[oq % 2]
                o_, or_ = obs[oq % 2]
                mk.op("dve", lambda: nc.vector.reciprocal(out=r_[64:128, :], in_=po[64:128, :]), reads=[por], writes=[rr_])
                mk.op("dve", lambda: nc.vector.tensor_tensor(out=o_[:], in0=po[0:64, :], in1=r_[64:128, :], op=ALU.mult),
                      reads=[por, rr_], writes=[or_])
                mk.dma("sp", out=m_d[512 + h * 64:512 + (h + 1) * 64, I * 512:(I + 1) * 512], in_=o_[:], reads=[or_])
                oq += 1
                pump(1, ("dve", "pool"))
        a.close()

    def run_stage(kind, l, src, dst, f_sb=None, f_r=None, which=0, with_out=False):
        a = Arena()
        tl_alloc(a, kind)

        def pre(t):
            sel(t)
            if kind == "ffn":
                if with_out:
                    outproj(l, t)
                ffn(l, which, "pre")
            elif kind == "ple":
                ple(l, t, "pre")
            else:
                rmsnorm_x(1, l)

        def loadx(t):
            sel(t)
            load_x(src, t)

        def storex(t):
            sel(t)
            store_x(dst, t)

        def main(t):
            def hook():
                if dst is not None and t >= 1:
                    storex(t - 1)
                if t + 2 < NT:
                    loadx(t + 2)
                sel(t)
            sel(t)
            TL["hook"] = hook
            if kind == "ffn":
                ffn(l, which, "main")
            elif kind == "ple":
                ple(l, t, "main")
            else:
                run_hook()
                if l % 2 == 0:
                    inproj_even(l, t, f_sb, f_r)
                else:
                    inproj_odd(l, t)
            run_hook()

        loadx(0)
        if NT > 1:
            loadx(1)
        pre(0)
        for t in range(NT):
            if t + 1 < NT:
                pre(t + 1)
            main(t)
        if dst is not None:
            storex(NT - 1)
        a.close()

    for l in range(L + 1):
        la = Arena()
        f_sb = la.sb("f_sb", [8, S], F32) if (mixers and l % 2 == 0 and l < L) else None
        f_r = Res("f")
        src = xT_in if l == 0 else xres_d
        if l > 0:
            run_stage("ffn", l - 1, src, xres_d, which=1, with_out=mixers)
            run_stage("ple", l - 1, xres_d, yT_out if l == L else xres_d)
            src = xres_d
        if l < L:
            run_stage("ffn", l, src, xres_d, which=0)
            if mixers:
                run_stage("inproj", l, xres_d, None, f_sb=f_sb, f_r=f_r)
        if l < L:
            enqueue_second_half(l)
            if l + 1 < L:
                enqueue_first_half(l + 1)
        if l < L and mixers:
            if l % 2 == 0:
                rglru_core(l)
                fox_core(l, f_sb, f_r)
            else:
                swa_core(l)
                s5_core(l)
        pump(10 ** 6)
        la.close()
    mk.barrier()
    glob.es.close()
    es.close()
    return nc


def host_prep(inputs, S=4096, L=4, mixers=True):
    f32 = np.float32
    L2 = (L + 1) // 2
    maps = []
    g = np.stack([inputs[n][:L] for n in ["ffn1_norm", "mix_norm", "ffn2_norm", "ple_norm", "ple_gate_norm"]], 0)
    gains = np.ascontiguousarray(g.reshape(5, L, 8, 128).transpose(3, 0, 1, 2)).astype(f32)
    shared = {"gains": gains}
    for nm in ["ffn1_wg", "ffn1_wu", "ffn2_wg", "ffn2_wu", "ffn1_wd", "ffn2_wd", "ple_w", "ple_gate_w"]:
        shared[nm] = np.ascontiguousarray(inputs[nm][:L])
    shared["ident"] = np.eye(128, dtype=f32)
    jj = np.arange(128)[:, None]
    ii = np.arange(128)[None, :]
    shared["negmask"] = np.where(jj <= ii, 0.0, -240000.0).astype(f32)
    if mixers:
        shared["ev_w_in"] = np.ascontiguousarray(inputs["ev_w_in"][:L2])
        shared["ev_w_out"] = np.ascontiguousarray(inputs["ev_w_out"][:L2])
        cols = [inputs["lru_conv_w"][:L2, t] for t in range(4)] + [inputs[n][:L2] for n in
                                                                  ["lru_conv_b", "lru_ba", "lru_bx", "lru_lambda"]]
        lp = np.stack(cols, -1)
        shared["lrup"] = np.ascontiguousarray(lp.reshape(L2, 4, 128, 8).transpose(2, 0, 1, 3)).astype(f32)
        wbd = np.zeros((L2, 2, 4, 128, 128), f32)
        for a, nm in enumerate(["lru_wa", "lru_wx"]):
            w = inputs[nm][:L2]
            for cc in range(4):
                wbd[:, a, cc, 0:64, 0:64] = w[:, 2 * cc]
                wbd[:, a, cc, 64:128, 64:128] = w[:, 2 * cc + 1]
        shared["lru_wbd"] = wbd
        fp = np.zeros((128, L2, 3), f32)
        fp[:, :, 0] = np.tile(inputs["fox_q_norm"][:L2], (1, 2)).T
        fp[:, :, 1] = np.tile(inputs["fox_k_norm"][:L2], (1, 2)).T
        fp[0:8, :, 2] = inputs["fox_bf"][:L2].T
        shared["foxp"] = fp
        LO = max(1, L // 2)
        shared["od_w_in"] = np.ascontiguousarray(inputs["od_w_in"][:LO])
        shared["od_w_out"] = np.ascontiguousarray(inputs["od_w_out"][:LO])
        sw = np.zeros((128, LO, 12), f32)
        for ci, nm in [(0, "swa_q_norm"), (2, "swa_k_norm")]:
            gq = inputs[nm][:LO]
            gsw = np.concatenate([gq[:, 32:], gq[:, :32]], 1)
            sw[:, :, ci] = np.tile(gq, (1, 2)).T
            sw[:, :, ci + 1] = np.tile(gsw, (1, 2)).T
        sw[:, :, 4:12] = inputs["swa_sinks"][:LO][None, :, :]
        shared["swap"] = sw
        half = 32
        inv = np.power(np.float32(10000.0), -np.arange(half, dtype=f32) / np.float32(half)).astype(f32)
        ang = (np.arange(S, dtype=f32)[None, :] * inv[:, None]).astype(f32)
        cosv = np.cos(ang.astype(np.float64)).astype(f32)
        sinv = np.sin(ang.astype(np.float64)).astype(f32)
        shared["ropec"] = np.ascontiguousarray(np.concatenate([cosv, cosv, cosv, cosv], 0))
        shared["ropes"] = np.ascontiguousarray(np.concatenate([-sinv, sinv, -sinv, sinv], 0))
        jj2 = np.arange(128)[:, None]
        ii2 = np.arange(256)[None, :]
        shared["bandmask"] = np.where(ii2 < 128, jj2 <= ii2, jj2 > ii2 - 128).astype(f32)
        tt = np.arange(S)
        shared["iota_ab"] = np.ascontiguousarray(np.broadcast_to(
            np.stack([tt // 64, tt % 64], 0).astype(f32)[None], (128, 2, S)))
        lamr = inputs["s5_lambda_re"][:LO].reshape(LO, 16, 2, 64)
        lami = inputs["s5_lambda_im"][:LO].reshape(LO, 16, 2, 64)
        ldt = np.broadcast_to(inputs["s5_log_dt"][:LO].reshape(LO, 16, 2, 1), (LO, 16, 2, 64))
        sl = np.stack([lamr, lami, ldt], -1)
        shared["s5lam"] = np.ascontiguousarray(sl.transpose(2, 3, 0, 1, 4).reshape(128, LO, 16, 3)).astype(f32)
        sB = np.zeros((LO, 2, 128, 16, 128), f32)
        sC = np.zeros((LO, 2, 128, 16, 128), f32)
        for ri, (bn, cn) in enumerate([("s5_b_re", "s5_c_re"), ("s5_b_im", "s5_c_im")]):
            bsrc = inputs[bn][:LO]
            csrc = inputs[cn][:LO]
            for g in range(32):
                sB[:, ri, (g % 2) * 64:(g % 2) * 64 + 64, g // 2, (g % 8) * 16:(g % 8) * 16 + 16] = bsrc[:, g]
                sC[:, ri, (g % 8) * 16:(g % 8) * 16 + 16, g // 2, (g % 2) * 64:(g % 2) * 64 + 64] = csrc[:, g]
        shared["s5B"] = sB
        shared["s5C"] = sC
        sd = np.stack([inputs["s5_d"][:LO], inputs["s5_glu_b"][:LO]], -1)
        shared["s5d"] = np.ascontiguousarray(sd.reshape(LO, 4, 128, 2).transpose(2, 0, 1, 3)).astype(f32)
        shared["s5_glu_w"] = np.ascontiguousarray(inputs["s5_glu_w"][:LO])
    B = inputs["x"].shape[0]
    for b in range(B):
        m = dict(shared)
        m["xT"] = np.ascontiguousarray(inputs["x"][b, :S].T)
        m["pT"] = np.ascontiguousarray(inputs["p"][:L, b, :S].transpose(0, 2, 1))
        maps.append(m)
    return maps


def kernel(**inputs):
    inputs = {k: np.asarray(v) for k, v in inputs.items()}
    nc = build()
    maps = host_prep(inputs)
    res = run_bass_kernel_spmd(nc, maps, core_ids=list(range(8)))
    out = np.stack([np.ascontiguousarray(r["yT"].T) for r in res.results], 0)
    return out.astype(np.float32)
```
